# Optimizing a Trainium2 kernel written in Bass

```python
import math
import jax, jax.numpy as jnp
from jax import lax
import numpy as np

D_MODEL = 1024
BATCH = 16
SEQ = 2048
DEPTH = 1

CTX_LEN = 256
GRID_W = 64
D_MIX = D_MODEL
D_ATTN = D_MIX // 2
D_CONV = D_MIX - D_ATTN
HEAD_DIM = 64
N_Q_HEADS = D_ATTN // HEAD_DIM
N_KV_HEADS = 2
GROUP = N_Q_HEADS // N_KV_HEADS
ROPE_AXIS_DIM = HEAD_DIM // 2
ROPE_THETA = 10000.0
Q_BLOCK = 128
CONV_WIDTH = 31
CONV_PAD = (CONV_WIDTH - 1) // 2
EPS = 1e-6

Q_LO, Q_HI = 0, D_ATTN
K_LO, K_HI = Q_HI, Q_HI + N_KV_HEADS * HEAD_DIM
V_LO, V_HI = K_HI, K_HI + N_KV_HEADS * HEAD_DIM
ZA_LO, ZA_HI = V_HI, V_HI + D_ATTN
GLU_LO, GLU_HI = ZA_HI, ZA_HI + 2 * D_CONV
ZC_LO, ZC_HI = GLU_HI, GLU_HI + D_CONV
D_IN = ZC_HI

kernel_name = "hybrid_gqa_conformer_prefix_dit_layer"


def rms_norm(x, w):
    xf = x.astype(jnp.float32)
    y = xf * lax.rsqrt(jnp.mean(xf * xf, axis=-1, keepdims=True) + EPS)
    return (y * w.astype(jnp.float32)).astype(x.dtype)


def layer_norm(x, w, b):
    xf = x.astype(jnp.float32)
    mu = jnp.mean(xf, axis=-1, keepdims=True)
    var = jnp.mean(jnp.square(xf - mu), axis=-1, keepdims=True)
    y = (xf - mu) * lax.rsqrt(var + EPS)
    return (y * w.astype(jnp.float32) + b.astype(jnp.float32)).astype(x.dtype)


def grid_angles(n_tokens, dtype):
    rows = n_tokens // GRID_W
    row = jnp.repeat(jnp.arange(rows, dtype=jnp.float32), GRID_W)
    col = jnp.tile(jnp.arange(GRID_W, dtype=jnp.float32), rows)
    freqs = ROPE_THETA ** (-jnp.arange(0, ROPE_AXIS_DIM, 2, dtype=jnp.float32) / ROPE_AXIS_DIM)
    ang_r = row[:, None] * freqs[None, :]
    ang_c = col[:, None] * freqs[None, :]
    return (jnp.cos(ang_r).astype(dtype), jnp.sin(ang_r).astype(dtype),
            jnp.cos(ang_c).astype(dtype), jnp.sin(ang_c).astype(dtype))


def rope_axis(x, cos, sin):
    cos = cos[None, :, None, :]
    sin = sin[None, :, None, :]
    x1, x2 = jnp.split(x, 2, axis=-1)
    return jnp.concatenate([x1 * cos - x2 * sin, x1 * sin + x2 * cos], axis=-1)


def rope_2d(x, angles):
    cr, sr, cc, sc = angles
    return jnp.concatenate([rope_axis(x[..., :ROPE_AXIS_DIM], cr, sr),
                            rope_axis(x[..., ROPE_AXIS_DIM:], cc, sc)], axis=-1)


def qk_heads(p, qn_w, kn_w):
    B, S, _ = p.shape
    q = p[..., Q_LO:Q_HI].reshape(B, S, N_Q_HEADS, HEAD_DIM)
    k = p[..., K_LO:K_HI].reshape(B, S, N_KV_HEADS, HEAD_DIM)
    v = p[..., V_LO:V_HI].reshape(B, S, N_KV_HEADS, HEAD_DIM)
    return rms_norm(q, qn_w), rms_norm(k, kn_w), v


def kv_heads(p_kv, kn_w):
    B, S, _ = p_kv.shape
    kv = p_kv.reshape(B, S, 2, N_KV_HEADS, HEAD_DIM)
    return rms_norm(kv[:, :, 0], kn_w), kv[:, :, 1]


def attend_blocked(q, k, v):
    B, S, _, _ = q.shape
    scale = 1.0 / math.sqrt(HEAD_DIM)
    qb = q.reshape(B, S // Q_BLOCK, Q_BLOCK, N_KV_HEADS, GROUP, HEAD_DIM).transpose(1, 0, 2, 3, 4, 5)

    def one_block(qblk):
        s = jnp.einsum('bqkgd,bskd->bkgqs', qblk, k).astype(jnp.float32) * scale
        pr = jax.nn.softmax(s, axis=-1).astype(v.dtype)
        return jnp.einsum('bkgqs,bskd->bqkgd', pr, v)

    o = lax.map(one_block, qb)
    return o.transpose(1, 0, 2, 3, 4, 5).reshape(B, S, D_ATTN)


def attend_dense(q, k, v):
    B, C, _, _ = q.shape
    qg = q.reshape(B, C, N_KV_HEADS, GROUP, HEAD_DIM)
    s = jnp.einsum('bqkgd,bskd->bkgqs', qg, k).astype(jnp.float32) / math.sqrt(HEAD_DIM)
    pr = jax.nn.softmax(s, axis=-1).astype(v.dtype)
    return jnp.einsum('bkgqs,bskd->bqkgd', pr, v).reshape(B, C, D_ATTN)


def conformer_conv(glu_in, conv_w, conv_b, ln_w, ln_b, w_pw, b_pw):
    a, g = jnp.split(glu_in, 2, axis=-1)
    u = a * jax.nn.sigmoid(g)
    y = lax.conv_general_dilated(u, conv_w[:, None, :].astype(u.dtype), window_strides=(1,),
                                 padding=[(CONV_PAD, CONV_PAD)],
                                 dimension_numbers=('NWC', 'WIO', 'NWC'),
                                 feature_group_count=D_CONV) + conv_b
    y = jax.nn.silu(layer_norm(y, ln_w, ln_b))
    return y @ w_pw + b_pw


def setup_inputs(seed: int = 0) -> dict:
    key = jax.random.key(seed)
    ks = jax.random.split(key, 20)
    nrm = jax.random.normal
    f32 = jnp.float32
    return {
        "x": nrm(ks[0], (BATCH, SEQ, D_MODEL), f32),
        "c": nrm(ks[1], (BATCH, D_MODEL), f32),
        "ctx": nrm(ks[2], (BATCH, CTX_LEN, D_MODEL), f32),
        "c_ctx": nrm(ks[3], (D_MODEL,), f32),
        "w_mod": nrm(ks[4], (DEPTH, D_MODEL, 3 * D_MODEL), f32) * D_MODEL ** -0.5,
        "b_mod": nrm(ks[5], (DEPTH, 3 * D_MODEL), f32) * 0.02,
        "norm_w": 1.0 + 0.05 * nrm(ks[6], (DEPTH, D_MODEL), f32),
        "w_in": nrm(ks[7], (DEPTH, D_MODEL, D_IN), f32) * D_MODEL ** -0.5,
        "q_norm_w": 1.0 + 0.05 * nrm(ks[8], (DEPTH, HEAD_DIM), f32),
        "k_norm_w": 1.0 + 0.05 * nrm(ks[9], (DEPTH, HEAD_DIM), f32),
        "conv_w": nrm(ks[10], (DEPTH, CONV_WIDTH, D_CONV), f32) * CONV_WIDTH ** -0.5,
        "conv_b": nrm(ks[11], (DEPTH, D_CONV), f32) * 0.02,
        "conv_ln_w": 1.0 + 0.05 * nrm(ks[12], (DEPTH, D_CONV), f32),
        "conv_ln_b": nrm(ks[13], (DEPTH, D_CONV), f32) * 0.02,
        "w_pw": nrm(ks[14], (DEPTH, D_CONV, D_CONV), f32) * D_CONV ** -0.5,
        "b_pw": nrm(ks[15], (DEPTH, D_CONV), f32) * 0.02,
        "w_out": nrm(ks[16], (DEPTH, D_MIX, D_MODEL), f32) * D_MIX ** -0.5,
    }


def reference(x, c, ctx, c_ctx, w_mod, b_mod, norm_w, w_in, q_norm_w, k_norm_w,
              conv_w, conv_b, conv_ln_w, conv_ln_b, w_pw, b_pw, w_out):
    S = x.shape[1]
    angles = grid_angles(S, x.dtype)
    h, hc = x, ctx
    for l in range(DEPTH):
        mod = jax.nn.silu(c) @ w_mod[l] + b_mod[l]
        shift, scale, gate = jnp.split(mod, 3, axis=-1)
        mod_c = jax.nn.silu(c_ctx) @ w_mod[l] + b_mod[l]
        shift_c, scale_c, gate_c = jnp.split(mod_c, 3, axis=-1)

        u = rms_norm(h, norm_w[l]) * (1.0 + scale[:, None, :]) + shift[:, None, :]
        uc = rms_norm(hc, norm_w[l]) * (1.0 + scale_c) + shift_c

        p = u @ w_in[l]
        q, k, v = qk_heads(p, q_norm_w[l], k_norm_w[l])
        q = rope_2d(q, angles)
        k = rope_2d(k, angles)

        if l == DEPTH - 1:
            k_c, v_c = kv_heads(uc @ w_in[l][:, K_LO:V_HI], k_norm_w[l])
        else:
            pc = uc @ w_in[l]
            q_c, k_c, v_c = qk_heads(pc, q_norm_w[l], k_norm_w[l])
            attn_c = attend_dense(q_c, k_c, v_c) * jax.nn.silu(pc[..., ZA_LO:ZA_HI])
            conv_c = conformer_conv(pc[..., GLU_LO:GLU_HI], conv_w[l], conv_b[l], conv_ln_w[l],
                                    conv_ln_b[l], w_pw[l], b_pw[l]) * jax.nn.silu(pc[..., ZC_LO:ZC_HI])
            out_c = jnp.concatenate([attn_c, conv_c], axis=-1) @ w_out[l]

        k_all = jnp.concatenate([k_c, k], axis=1)
        v_all = jnp.concatenate([v_c, v], axis=1)
        attn = attend_blocked(q, k_all, v_all) * jax.nn.silu(p[..., ZA_LO:ZA_HI])

        conv = conformer_conv(p[..., GLU_LO:GLU_HI], conv_w[l], conv_b[l], conv_ln_w[l],
                              conv_ln_b[l], w_pw[l], b_pw[l]) * jax.nn.silu(p[..., ZC_LO:ZC_HI])

        out = jnp.concatenate([attn, conv], axis=-1) @ w_out[l]
        h = h + gate[:, None, :] * out
        if l < DEPTH - 1:
            hc = hc + gate_c * out_c
    return h
```

```python
import contextlib
import numpy as np
import concourse.bass as bass
import concourse.mybir as mybir
from concourse.bass_utils import run_bass_kernel_spmd

F32 = mybir.dt.float32
BF16 = mybir.dt.bfloat16
AF = mybir.ActivationFunctionType
ALU = mybir.AluOpType

ENGS = ["pe", "act", "dve", "pool", "sp"]
import os
SAME_ENGINE_SYNC = not os.environ.get("NOSES")
EPS = 1e-6
S_LAT = 2048
C_LEN = 256
NKT = (S_LAT + C_LEN) // 128

V_NORMW = 0
V_BMOD = 8
V_QNW = 32
V_KNW = 33
V_CB = 34
V_LNW = 38
V_LNB = 42
V_BPW = 46
V_CW = 50
NV = 50 + 124


class Buf:
    __slots__ = ("name", "w", "r")

    def __init__(self, name=""):
        self.name = name
        self.w = None
        self.r = []


def I(fn, *a, **k):
    return lambda: fn(*a, **k)


class Prog:
    def __init__(self):
        self.ops = {e: [] for e in ENGS}
        self.cnt = {}
        self.waited = {e: {} for e in ENGS}
        self.semkeys = []
        self.glob = []
        self.enabled = True
        self.stage = 0
        self.stop_stage = 10 ** 9

    def sub(self, x):
        if self.enabled and self.base + x > self.stop_stage:
            self.enabled = False

    def mark(self, n):
        self.base = n
        self.stage = n
        if n > self.stop_stage:
            self.enabled = False

    def _sem(self, key):
        if key not in self.cnt:
            self.cnt[key] = 0
            self.semkeys.append(key)

    def _waits(self, eng, reads, writes, extra):
        deps = {}

        def add(tok):
            if tok is None:
                return
            k, v = tok
            if deps.get(k, 0) < v:
                deps[k] = v
        for b in reads:
            add(b.w)
        for b in writes:
            add(b.w)
            for t in b.r:
                add(t)
        for t in extra:
            add(t)
        out = []
        for k, v in deps.items():
            if k == eng and not SAME_ENGINE_SYNC:
                continue
            if self.waited[eng].get(k, 0) >= v:
                continue
            self.waited[eng][k] = v
            out.append((k, v))
        return out

    def emit(self, eng, fns, reads=(), writes=(), extra=()):
        if not self.enabled:
            return None
        if callable(fns):
            fns = [fns]
        self._sem(eng)
        waits = self._waits(eng, reads, writes, extra)
        self.cnt[eng] += 1
        tok = (eng, self.cnt[eng])
        self.glob.append((eng, (waits, list(fns), (eng, 1))))
        for b in reads:
            b.r.append(tok)
        for b in writes:
            b.w = tok
            b.r = []
        return tok

    def dma(self, queue, fn, semkey, reads=(), writes=(), extra=()):
        if not self.enabled:
            return None
        self._sem(semkey)
        waits = self._waits(queue, reads, writes, extra)
        self.cnt[semkey] += 16
        tok = (semkey, self.cnt[semkey])
        self.glob.append((queue, (waits, [fn], (semkey, 16))))
        for b in reads:
            b.r.append(tok)
        for b in writes:
            b.w = tok
            b.r = []
        return tok

    def wait_only(self, eng, toks):
        waits = self._waits(eng, (), (), toks)
        if waits:
            self.glob.append((eng, (waits, [], None)))

    def build(self, nc, stack, max_ins=10 ** 9):
        sems = {}
        for k in self.semkeys:
            sems[k] = stack.enter_context(nc.semaphore("s_" + k))
        chunks = []
        cur = {e: [] for e in ENGS}
        cnt = {e: 0 for e in ENGS}
        for eng, op in self.glob:
            n = len(op[0]) + len(op[1])
            if cnt[eng] + n > max_ins and cnt[eng] > 0:
                chunks.append(cur)
                cur = {e: [] for e in ENGS}
                cnt = {e: 0 for e in ENGS}
            cur[eng].append(op)
            cnt[eng] += n
        chunks.append(cur)
        self.nchunks = len(chunks)

        def make(ops):
            def body(engine):
                for waits, fns, inc in ops:
                    for k, v in waits:
                        engine.wait_ge(sems[k], v)
                    for i, fn in enumerate(fns):
                        ins = fn()
                        if i == len(fns) - 1 and inc is not None:
                            ins.then_inc(sems[inc[0]], inc[1])
            return body

        for ch in chunks:
            with nc.Block() as block:
                engdec = {"pe": block.tensor, "act": block.scalar, "dve": block.vector,
                          "pool": block.gpsimd, "sp": block.sync}
                for e in ENGS:
                    if ch[e]:
                        engdec[e](make(ch[e]))


class RR:
    def __init__(self, n):
        self.n = n
        self.i = -1

    def next(self):
        self.i = (self.i + 1) % self.n
        return self.i


def build_nc(dbg=False, stop_stage=10 ** 9):
    nc = bass.Bass("TRN2", target_bir_lowering=False)
    x_d = nc.dram_tensor("x", [2, S_LAT, 1024], F32, kind="ExternalInput").ap()
    ctx_d = nc.dram_tensor("ctx", [2, C_LEN, 1024], F32, kind="ExternalInput").ap()
    cT_d = nc.dram_tensor("cT", [128, 24], F32, kind="ExternalInput").ap()
    wmod_d = nc.dram_tensor("w_mod", [1024, 3072], F32, kind="ExternalInput").ap()
    win_d = nc.dram_tensor("w_in", [1024, 2816], F32, kind="ExternalInput").ap()
    wpw_d = nc.dram_tensor("w_pw", [512, 512], F32, kind="ExternalInput").ap()
    wout_d = nc.dram_tensor("w_out", [1024, 1024], F32, kind="ExternalInput").ap()
    vecs_d = nc.dram_tensor("vecs", [128, NV], F32, kind="ExternalInput").ap()
    consts_d = nc.dram_tensor("consts", [128, 512], F32, kind="ExternalInput").ap()
    cos_d = nc.dram_tensor("cosT", [128, S_LAT], F32, kind="ExternalInput").ap()
    sin_d = nc.dram_tensor("sinT", [128, S_LAT], F32, kind="ExternalInput").ap()
    out_d = nc.dram_tensor("out", [2, S_LAT, 1024], F32, kind="ExternalOutput").ap()
    dbg_d = {}
    if dbg:
        for name, shp in [("d_uT", [128, 8 * 512]), ("d_KT", [128, 2304]), ("d_V", [128, NKT * 256]),
                          ("d_QT", [128, 8 * 512]), ("d_za", [128, 4 * 512]), ("d_zc", [128, 4 * 512]),
                          ("d_uc", [128, 4 * 2078]), ("d_AB", [128, 48])]:
            dbg_d[name] = nc.dram_tensor(name, shp, F32, kind="ExternalOutput").ap()

    P = Prog()
    P.stop_stage = stop_stage
    st = contextlib.ExitStack()
    with st:
        def sb(name, shape, dt):
            return st.enter_context(nc.sbuf_tensor("sb_" + name, shape, dt))

        def ps(name, shape, dt):
            return st.enter_context(nc.psum_tensor("ps_" + name, shape, dt))

        win = sb("win", [128, 8, 2816], BF16)
        b_win = [Buf() for _ in range(4)]
        wout = sb("wout", [128, 8, 1024], BF16)
        b_wout = [Buf() for _ in range(8)]
        wpw = sb("wpw", [128, 4, 512], BF16)
        b_wpw = Buf()
        diag = [sb(f"diag{i}", [128, 31, 128], BF16) for i in range(2)]
        b_diag = [Buf() for _ in range(2)]
        rr_diag = RR(2)
        uT = sb("uT", [128, 8, 512], BF16)
        b_uT = [Buf() for _ in range(8)]
        xn = sb("xn", [128, 4, 1024], BF16)
        b_xn = [Buf() for _ in range(4)]
        xt = [sb(f"xt{i}", [128, 1024], F32) for i in range(3)]
        b_xt = [Buf() for _ in range(3)]
        rr_xt = RR(3)
        QT = sb("QT", [128, 4, 2, 512], BF16)
        b_QT = [Buf() for _ in range(4)]
        KT = sb("KT", [128, 2304], BF16)
        b_KT = [Buf() for _ in range(5)]
        Vaug = sb("Vaug", [128, NKT, 2, 128], BF16)
        b_V = [Buf() for _ in range(NKT)]
        za = [sb(f"za{i}", [128, 4, 512], BF16) for i in range(2)]
        b_za = [[Buf() for _ in range(4)] for _ in range(2)]
        zc = [sb(f"zc{i}", [128, 4, 512], BF16) for i in range(2)]
        b_zc = [[Buf() for _ in range(4)] for _ in range(2)]
        uconv = sb("uconv", [128, 4, 2078], BF16)
        b_uc = [[Buf() for _ in range(4)] for _ in range(4)]
        b_ucpad = Buf()
        NPT = 3
        PT = [sb(f"PT{i}", [128, 1024], BF16) for i in range(NPT)]
        b_PT = [Buf() for _ in range(NPT)]
        rr_PT = RR(NPT)
        NF = 8
        fs = [sb(f"fs{i}", [128, 512], F32) for i in range(NF)]
        b_fs = [Buf() for _ in range(NF)]
        rr_fs = RR(NF)
        NB = 6
        bs = [sb(f"bs{i}", [128, 512], BF16) for i in range(NB)]
        b_bs = [Buf() for _ in range(NB)]
        rr_bs = RR(NB)
        ybf = sb("ybf", [128, 4, 512], BF16)
        b_ybf = [Buf() for _ in range(4)]
        y2 = sb("y2", [128, 4, 512], BF16)
        y2f = y2[:].rearrange("p a b -> p (a b)")
        b_y2 = [Buf() for _ in range(4)]
        xres = [sb(f"xres{i}", [128, 512], F32) for i in range(2)]
        b_xres = [Buf() for _ in range(2)]
        rr_xres = RR(2)
        cosb, sinb = xres[0], xres[1]
        b_cos, b_sin = b_xres[0], b_xres[1]
        cbf = sb("cbf", [128, 512], BF16)
        b_cbf = Buf()
        ident = cbf[:, 0:128]
        Rm = cbf[:, 128:256]
        bones = cbf[:, 256:384]
        odiv = cbf[:, 384:512]
        vecs = sb("vecs", [128, NV], F32)
        b_vecs = Buf()
        cT = sb("cT", [128, 24], F32)
        b_cT = Buf()
        scT = sb("scT", [128, 24], BF16)
        b_scT = Buf()
        modsb = sb("modsb", [128, 24, 3], F32)
        b_modsb = Buf()
        Asb = sb("Asb", [128, 8, 3], F32)
        b_A = Buf()
        NST = 8
        stat = sb("stat", [128, NST, 4], F32)
        b_stat = [Buf() for _ in range(NST)]
        rr_stat = RR(NST)

        NG = 2
        gen = [ps(f"gen{i}", [128, 512], F32) for i in range(NG)]
        b_gen = [Buf() for _ in range(NG)]
        rr_gen = RR(NG)
        Sb = [ps(f"S{i}", [128, 1024], F32) for i in range(2)]
        b_S = [Buf() for _ in range(2)]
        rr_S = RR(2)
        Tbs = [Sb[0][:, 0:512].bitcast(BF16), Sb[1][:, 0:512].bitcast(BF16)]
        b_T = b_S
        rr_T = RR(2)
        Ob = [ps(f"O{i}", [128, 512], F32) for i in range(2)]
        b_O = [Buf() for _ in range(2)]
        rr_O = RR(2)

        def vcol(off, n=1):
            return vecs[:, off:off + n]

        P.dma("sp", I(nc.sync.dma_start, out=vecs[:], in_=vecs_d[:, :]), "c_vecs", writes=[b_vecs])
        P.dma("sp", I(nc.sync.dma_start, out=cT[:], in_=cT_d[:, :]), "c_cT", writes=[b_cT])
        P.dma("pool", I(nc.gpsimd.dma_start, out=cbf[:], in_=consts_d[:, :]), "c_cbf", writes=[b_cbf])
        win_v = win_d.rearrange("(kc p) n -> p kc n", p=128)
        pieces = [(512, 768), (0, 512), (768, 1792), (1792, 2816)]
        def load_w(i):
            lo, hi = pieces[i]
            P.dma("pool", I(nc.gpsimd.dma_start, out=win[:, :, lo:hi], in_=win_v[:, :, lo:hi]), f"w_in{i}",
                  writes=[b_win[i]], extra=wmod_toks)
        wmod_toks = []
        load_w(0)
        P.emit("pool", I(nc.gpsimd.memset, uconv[:, :, 0:15], 0.0), writes=[b_ucpad])
        P.emit("pool", I(nc.gpsimd.memset, uconv[:, :, 2063:2078], 0.0), writes=[b_ucpad])
        P.emit("pool", I(nc.gpsimd.memset, Vaug[:], 1.0), writes=b_V)
        P.emit("pool", I(nc.gpsimd.memset, QT[:], 0.0), writes=b_QT)

        P.mark(1)
        P.emit("act", I(nc.scalar.activation, out=scT[:], in_=cT[:], func=AF.Silu), reads=[b_cT], writes=[b_scT])
        scT3 = scT[:].rearrange("p (k j) -> p k j", j=3)
        modps = gen[0]
        first = True
        wmod_last = {}
        for kc in range(8):
            for third in range(3):
                s = rr_xt.next()
                xtb = xt[s][:].bitcast(BF16)[:, 0:1024]
                tk = P.dma("pool", I(nc.gpsimd.dma_start, out=xtb, in_=wmod_d[kc * 128:(kc + 1) * 128, third * 1024:(third + 1) * 1024]),
                           f"wm{s}", writes=[b_xt[s]])
                wmod_last[s] = tk
                fns = []
                for jj in range(8):
                    col = (third * 8 + jj) * 3
                    fns.append(I(nc.tensor.matmul, modps[:, col:col + 3], lhsT=xtb[:, jj * 128:(jj + 1) * 128],
                                 rhs=scT3[:, kc, :], start=first, stop=(kc == 7 and third == 2 and jj == 7),
                                 skip_group_check=True))
                    first = False
                P.emit("pe", fns, reads=[b_xt[s], b_scT], writes=[b_gen[0]])
        wmod_toks = list(wmod_last.values())
        for i in (1, 2, 3):
            load_w(i)
        P.dma("pool", I(nc.gpsimd.dma_start, out=wpw[:], in_=wpw_d.rearrange("(cc p) n -> p cc n", p=128)),
              "w_pw", writes=[b_wpw])
        P.mark(2)
        P.emit("dve", I(nc.vector.tensor_tensor, out=modsb[:], in0=modps[:, 0:72].rearrange("p (c j) -> p c j", j=3),
                        in1=vcol(V_BMOD, 24).unsqueeze(2).to_broadcast([128, 24, 3]), op=ALU.add),
               reads=[b_gen[0], b_vecs], writes=[b_modsb])
        P.emit("dve", I(nc.vector.scalar_tensor_tensor, out=Asb[:], in0=modsb[:, 8:16, :], scalar=1.0,
                        in1=vcol(V_NORMW, 8).unsqueeze(2).to_broadcast([128, 8, 3]), op0=ALU.add, op1=ALU.mult),
               reads=[b_modsb, b_vecs], writes=[b_A])
        def Acol(kc, j):
            return Asb[:, kc, j:j + 1]

        def Bcol(kc, j):
            return modsb[:, kc, j:j + 1]

        def xphase(srcs, sq_eng="dve", rstd_eng="act"):
            for tt, src in enumerate(srcs):
                s = rr_xt.next()
                P.dma("sp", I(nc.sync.dma_start, out=xt[s][:], in_=src), f"xt{s}", writes=[b_xt[s]])
                q = rr_stat.next()
                if sq_eng == "act" or (sq_eng == "mix" and tt % 2 == 0):
                    P.emit("act", I(nc.scalar.activation, out=y2f[:, 0:1024], in_=xt[s][:], func=AF.Square,
                                    accum_out=stat[:, q, 0:1]), reads=[b_xt[s]], writes=b_y2[0:2] + [b_stat[q]])
                else:
                    P.emit("dve", I(nc.vector.scalar_tensor_tensor, out=y2f[:, 0:1024], in0=xt[s][:], scalar=1.0, in1=xt[s][:],
                                    op0=ALU.mult, op1=ALU.mult, accum_out=stat[:, q, 0:1]),
                           reads=[b_xt[s]], writes=b_y2[0:2] + [b_stat[q]])
                if rstd_eng == "pool":
                    P.emit("pool", I(nc.gpsimd.tensor_scalar, out=stat[:, q, 1:2], in0=stat[:, q, 0:1], scalar1=1.0 / 1024.0,
                                     scalar2=EPS, op0=ALU.mult, op1=ALU.add), reads=[b_stat[q]], writes=[b_stat[q]])
                    P.emit("pool", I(nc.gpsimd.tensor_tensor, out=stat[:, q, 2:3], in0=stat[:, q, 1:2], in1=epsb[:, 1:2], op=ALU.pow),
                           reads=[b_stat[q], b_eps], writes=[b_stat[q]])
                else:
                    P.emit("act", I(nc.scalar.activation, out=stat[:, q, 1:2], in_=stat[:, q, 0:1], func=AF.Ln,
                                    scale=1.0 / 1024.0, bias=epsb[:, 0:1]), reads=[b_stat[q], b_eps], writes=[b_stat[q]])
                    P.emit("act", I(nc.scalar.activation, out=stat[:, q, 2:3], in_=stat[:, q, 1:2], func=AF.Exp,
                                    scale=-0.5), reads=[b_stat[q]], writes=[b_stat[q]])
                P.emit("pool", I(nc.gpsimd.tensor_scalar, out=xn[:, tt, :], in0=xt[s][:], scalar1=stat[:, q, 2:3],
                                 scalar2=0.0, op0=ALU.mult, op1=ALU.add),
                       reads=[b_xt[s], b_stat[q]], writes=[b_xn[tt]])

        def tphase(nt, j):
            for kc in range(8):
                g = rr_T.next()
                tb = Tbs[g]
                fns = [I(nc.tensor.transpose, out=tb[:, tt * 128:(tt + 1) * 128], in_=xn[:, tt, kc * 128:(kc + 1) * 128],
                         identity=ident) for tt in range(nt)]
                P.emit("pe", fns, reads=b_xn[:nt] + [b_cbf], writes=[b_T[g]])
                P.emit("dve", I(nc.vector.tensor_scalar, out=uT[:, kc, 0:nt * 128], in0=tb[:, 0:nt * 128],
                                scalar1=Acol(kc, j), scalar2=Bcol(kc, j), op0=ALU.mult, op1=ALU.add),
                       reads=[b_T[g], b_A, b_modsb], writes=[b_uT[kc]])

        def proj(lo, N, wbufs):
            g = rr_gen.next()
            fns = [I(nc.tensor.matmul, gen[g][:, 0:N], lhsT=win[:, kc, lo:lo + 128], rhs=uT[:, kc, 0:N],
                     start=(kc == 0), stop=(kc == 7)) for kc in range(8)]
            P.emit("pe", fns, reads=b_uT + wbufs, writes=[b_gen[g]])
            return g

        def qk_A(g, N, L):
            P.emit("dve", I(nc.vector.tensor_copy, out=fs[L][:, 0:N], in_=gen[g][:, 0:N]),
                   reads=[b_gen[g]], writes=[b_fs[L]])
            P.emit("pool", I(nc.gpsimd.tensor_tensor, out=bs[L][:, 0:N], in0=fs[L][:, 0:N], in1=fs[L][:, 0:N], op=ALU.mult),
                   reads=[b_fs[L]], writes=[b_bs[L]])

        def qk_B(N, L):
            m = rr_gen.next()
            P.emit("pe", I(nc.tensor.matmul, gen[m][:, 0:N], lhsT=bones, rhs=bs[L][:, 0:N], start=True, stop=True),
                   reads=[b_bs[L], b_cbf], writes=[b_gen[m]])
            P.emit("act", I(nc.scalar.activation, out=fs[4 + L][:, 0:N], in_=gen[m][:, 0:N], func=AF.Ln, bias=epsb[:, 0:1]),
                   reads=[b_gen[m], b_eps], writes=[b_fs[4 + L]])
            P.emit("act", I(nc.scalar.activation, out=fs[4 + L][:, 0:N], in_=fs[4 + L][:, 0:N], func=AF.Exp, scale=-0.5),
                   reads=[b_fs[4 + L]], writes=[b_fs[4 + L]])

        def qk_C(N, L, wcol, rope, dests):
            raw, r = L, 4 + L
            if not rope:
                for rows, dest, dbuf in dests:
                    P.emit("dve", I(nc.vector.scalar_tensor_tensor, out=dest, in0=fs[raw][rows, 0:N], scalar=wcol[rows, :],
                                    in1=fs[r][rows, 0:N], op0=ALU.mult, op1=ALU.mult),
                           reads=[b_fs[raw], b_fs[r], b_vecs], writes=[dbuf])
                return
            P.emit("dve", I(nc.vector.scalar_tensor_tensor, out=bs[L][:, 0:N], in0=fs[raw][:, 0:N], scalar=wcol,
                            in1=fs[r][:, 0:N], op0=ALU.mult, op1=ALU.mult),
                   reads=[b_fs[raw], b_fs[r], b_vecs], writes=[b_bs[L]])
            m = rr_gen.next()
            P.emit("pe", I(nc.tensor.matmul, gen[m][:, 0:N], lhsT=Rm, rhs=bs[L][:, 0:N], start=True, stop=True),
                   reads=[b_bs[L], b_cbf], writes=[b_gen[m]])
            P.emit("pool", I(nc.gpsimd.tensor_tensor, out=fs[raw][:, 0:N], in0=bs[L][:, 0:N], in1=cosb[:, 0:N], op=ALU.mult),
                   reads=[b_bs[L], b_cos], writes=[b_fs[raw]])
            P.emit("dve", I(nc.vector.tensor_tensor, out=fs[r][:, 0:N], in0=gen[m][:, 0:N], in1=sinb[:, 0:N], op=ALU.mult),
                   reads=[b_gen[m], b_sin], writes=[b_fs[r]])
            for rows, dest, dbuf in dests:
                P.emit("pool", I(nc.gpsimd.tensor_tensor, out=dest, in0=fs[raw][rows, 0:N], in1=fs[r][rows, 0:N], op=ALU.add),
                       reads=[b_fs[raw], b_fs[r]], writes=[dbuf])

        def load_rope(g):
            P.dma("sp", I(nc.sync.dma_start, out=cosb[:], in_=cos_d[:, g * 512:(g + 1) * 512]), "cosb", writes=[b_cos])
            P.dma("sp", I(nc.sync.dma_start, out=sinb[:], in_=sin_d[:, g * 512:(g + 1) * 512]), "sinb", writes=[b_sin])

        def kv_front(nt, j, koff, L, after_t=None):
            N = nt * 128
            tphase(nt, j)
            if after_t is not None:
                after_t()
            g = proj(512, N, [b_win[0]])
            qk_A(g, N, L)
            qk_B(N, L)
            for tt in range(nt):
                gv = rr_gen.next()
                fns = [I(nc.tensor.matmul, gen[gv][:, 0:128], lhsT=uT[:, kc, tt * 128:(tt + 1) * 128],
                         rhs=win[:, kc, 640:768], start=(kc == 0), stop=(kc == 7)) for kc in range(8)]
                P.emit("pe", fns, reads=b_uT + [b_win[0]], writes=[b_gen[gv]])
                tile = koff // 128 + tt
                P.emit("act", I(nc.scalar.copy, out=Vaug[:, tile, 0, 0:64], in_=gen[gv][:, 0:64]),
                       reads=[b_gen[gv]], writes=[b_V[tile]])
                P.emit("act", I(nc.scalar.copy, out=Vaug[:, tile, 1, 64:128], in_=gen[gv][:, 64:128]),
                       reads=[b_gen[gv]], writes=[b_V[tile]])

        def kv_back(nt, koff, kidx, rope, L):
            N = nt * 128
            qk_C(N, L, vcol(V_KNW), rope, [(slice(0, 128), KT[:, koff:koff + N], b_KT[kidx])])

        def attention(gp, hook=None, bg=()):
            NP = NKT // 2
            items = [(c, half, pr) for c in range(4) for half in range(2) for pr in range(NP)]
            sbank = {}

            def emitS(idx):
                c, half, pr = items[idx]
                s = rr_S.next()
                sbank[idx] = s
                fns = []
                kset = set()
                for e in range(2):
                    kt = 2 * pr + e
                    kset.add(0 if kt < 2 else 1 + (kt - 2) // 4)
                    fns.append(I(nc.tensor.matmul, Sb[s][:, e * 512:(e + 1) * 512], lhsT=KT[:, kt * 128:(kt + 1) * 128],
                                 rhs=QT[:, c, half, :], start=True, stop=True))
                P.emit("pe", fns, reads=[b_KT[k] for k in kset] + [b_QT[c]], writes=[b_S[s]])
            SFIRST = not os.environ.get("NOSFIRST")
            emitS(0)
            if SFIRST:
                emitS(1)
            o = None
            bg = list(bg)
            bg_every = max(1, (len(items) - 8) // max(1, len(bg))) if bg else 0
            for idx, (c, half, pr) in enumerate(items):
                if idx == 8 and hook is not None:
                    hook()
                if bg and idx >= 4 and (idx - 4) % bg_every == 0:
                    bg.pop(0)()
                if not SFIRST and idx + 1 < len(items):
                    emitS(idx + 1)
                s = sbank.pop(idx)
                p = rr_PT.next()
                P.emit("act", I(nc.scalar.activation, out=PT[p][:, :], in_=Sb[s][:, :], func=AF.Exp, scale=0.125),
                       reads=[b_S[s]], writes=[b_PT[p]])
                if pr == 0:
                    o = rr_O.next()
                fns = []
                for e in range(2):
                    kt = 2 * pr + e
                    fns.append(I(nc.tensor.matmul, Ob[o][:, :], lhsT=Vaug[:, kt, half, :], rhs=PT[p][:, e * 512:(e + 1) * 512],
                                 start=(kt == 0), stop=(kt == NKT - 1)))
                if SFIRST and idx + 2 < len(items):
                    emitS(idx + 2)
                P.emit("pe", fns, reads=[b_V[2 * pr], b_V[2 * pr + 1], b_PT[p]], writes=[b_O[o]])
                if pr == NP - 1:
                    orow = slice(half * 64, half * 64 + 64)
                    srow = slice((1 - half) * 64, (1 - half) * 64 + 64)
                    ri = rr_fs.next()
                    P.emit("dve", I(nc.vector.reciprocal, out=fs[ri][srow, :], in_=Ob[o][srow, :]),
                           reads=[b_O[o]], writes=[b_fs[ri]])
                    ot = rr_fs.next()
                    P.emit("dve", I(nc.vector.tensor_tensor, out=fs[ot][orow, :], in0=Ob[o][orow, :], in1=fs[ri][srow, :],
                                    op=ALU.mult), reads=[b_O[o], b_fs[ri]], writes=[b_fs[ot]])
                    P.emit("pool", I(nc.gpsimd.tensor_tensor, out=za[gp][orow, c, :], in0=fs[ot][orow, :],
                                     in1=za[gp][orow, c, :], op=ALU.mult), reads=[b_fs[ot]], writes=[b_za[gp][c]])
            while bg:
                bg.pop(0)()

        cst = {}

        def build_diag(cc):
            d = rr_diag.next()
            fns = [I(nc.vector.tensor_scalar, out=diag[d][:, j, :], in0=ident, scalar1=vcol(V_CW + cc * 31 + j),
                     scalar2=None, op0=ALU.mult) for j in range(31)]
            P.emit("dve", fns, reads=[b_cbf, b_vecs], writes=[b_diag[d]])
            return d

        def prebuild_diags():
            cst["prebuilt"] = {0: build_diag(0), 1: build_diag(1)}

        def conv_block(gg, gp):
            conv_p1(gg)
            conv_p2a()
            conv_p2b(gp)

        def conv_p1(gg):
            mu_ps, msq_ps = Ob[0], Ob[1]
            for cc in range(4):
                if cc in cst.get("prebuilt", {}):
                    d = cst["prebuilt"].pop(cc)
                else:
                    d = build_diag(cc)
                g = rr_gen.next()
                fns = [I(nc.tensor.matmul, gen[g][:, :], lhsT=diag[d][:, j, :],
                         rhs=uconv[:, cc, gg * 512 + j: gg * 512 + j + 512], start=(j == 0), stop=(j == 30)) for j in range(31)]
                ucr = [b_uc[cc][k] for k in range(max(0, gg - 1), min(3, gg + 1) + 1)] + [b_ucpad]
                P.emit("pe", fns, reads=[b_diag[d]] + ucr, writes=[b_gen[g]])
                P.emit("act", I(nc.scalar.activation, out=ybf[:, cc, :], in_=gen[g][:, :], func=AF.Identity,
                                bias=vcol(V_CB + cc)), reads=[b_gen[g], b_vecs], writes=[b_ybf[cc]])
                sq = rr_bs.next()
                P.emit("pool", I(nc.gpsimd.tensor_tensor, out=bs[sq][:, :], in0=ybf[:, cc, :], in1=ybf[:, cc, :], op=ALU.mult),
                       reads=[b_ybf[cc]], writes=[b_bs[sq]])
                P.emit("pe", I(nc.tensor.matmul, mu_ps[:, :], lhsT=odiv, rhs=ybf[:, cc, :], start=(cc == 0), stop=(cc == 3)),
                       reads=[b_ybf[cc], b_cbf], writes=[b_O[0]])
                P.emit("pe", I(nc.tensor.matmul, msq_ps[:, :], lhsT=odiv, rhs=bs[sq][:, :], start=(cc == 0), stop=(cc == 3)),
                       reads=[b_bs[sq], b_cbf], writes=[b_O[1]])

        def conv_p2a():
            mu_ps, msq_ps = Ob[0], Ob[1]
            mu = rr_fs.next()
            P.emit("act", I(nc.scalar.copy, out=fs[mu][:, :], in_=mu_ps[:, :]), reads=[b_O[0]], writes=[b_fs[mu]])
            m2 = rr_fs.next()
            cst["mu"], cst["m2"] = mu, m2
            P.emit("pool", I(nc.gpsimd.tensor_tensor, out=fs[m2][:, :], in0=fs[mu][:, :], in1=fs[mu][:, :], op=ALU.mult),
                   reads=[b_fs[mu]], writes=[b_fs[m2]])
            P.emit("dve", I(nc.vector.tensor_tensor, out=fs[m2][:, :], in0=msq_ps[:, :], in1=fs[m2][:, :], op=ALU.subtract),
                   reads=[b_O[1], b_fs[m2]], writes=[b_fs[m2]])
            P.emit("act", I(nc.scalar.activation, out=fs[m2][:, :], in_=fs[m2][:, :], func=AF.Ln, bias=epsb[:, 0:1]),
                   reads=[b_fs[m2], b_eps], writes=[b_fs[m2]])
            P.emit("act", I(nc.scalar.activation, out=fs[m2][:, :], in_=fs[m2][:, :], func=AF.Exp, scale=-0.5),
                   reads=[b_fs[m2]], writes=[b_fs[m2]])

        def conv_p2b(gp):
            mu, m2 = cst["mu"], cst["m2"]
            g0, g1 = rr_gen.next(), rr_gen.next()
            banks = [(gen[g0], b_gen[g0]), (gen[g1], b_gen[g1]), (Ob[0], b_O[0]), (Ob[1], b_O[1])]
            for cc in range(4):
                t = rr_fs.next()
                P.emit("dve", I(nc.vector.tensor_tensor, out=fs[t][:, :], in0=ybf[:, cc, :], in1=fs[mu][:, :], op=ALU.subtract),
                       reads=[b_ybf[cc], b_fs[mu]], writes=[b_fs[t]])
                P.emit("dve", I(nc.vector.tensor_tensor, out=fs[t][:, :], in0=fs[t][:, :], in1=fs[m2][:, :], op=ALU.mult),
                       reads=[b_fs[t], b_fs[m2]], writes=[b_fs[t]])
                P.emit("act", I(nc.scalar.activation, out=y2[:, cc, :], in_=fs[t][:, :], func=AF.Silu,
                                scale=vcol(V_LNW + cc), bias=vcol(V_LNB + cc)), reads=[b_fs[t], b_vecs], writes=[b_y2[cc]])
                fns = [I(nc.tensor.matmul, banks[nn][0][:, :], lhsT=wpw[:, cc, nn * 128:(nn + 1) * 128], rhs=y2[:, cc, :],
                         start=(cc == 0), stop=(cc == 3)) for nn in range(4)]
                P.emit("pe", fns, reads=[b_y2[cc], b_wpw], writes=[bk[1] for bk in banks])
            for nn in range(4):
                P.emit("dve", I(nc.vector.scalar_tensor_tensor, out=zc[gp][:, nn, :], in0=banks[nn][0][:, :], scalar=vcol(V_BPW + nn),
                                in1=zc[gp][:, nn, :], op0=ALU.add, op1=ALU.mult),
                       reads=[banks[nn][1], b_vecs], writes=[b_zc[gp][nn]])

        store_toks = {}

        def outproj(bi, gg, gp):
            for tt in range(4):
                r0 = gg * 512 + tt * 128
                for nh in range(2):
                    xr = rr_xres.next()
                    P.dma("sp", I(nc.sync.dma_start, out=xres[xr][:], in_=x_d[bi, r0:r0 + 128, nh * 512:(nh + 1) * 512]),
                          f"xr{xr}", writes=[b_xres[xr]])
                    g = rr_gen.next()
                    fns = []
                    for dch in range(8):
                        src = za[gp][:, dch, tt * 128:(tt + 1) * 128] if dch < 4 else zc[gp][:, dch - 4, tt * 128:(tt + 1) * 128]
                        fns.append(I(nc.tensor.matmul, gen[g][:, :], lhsT=src, rhs=wout[:, dch, nh * 512:(nh + 1) * 512],
                                     start=(dch == 0), stop=(dch == 7)))
                    P.emit("pe", fns, reads=b_za[gp] + b_zc[gp] + b_wout, writes=[b_gen[g]])
                    P.emit("dve", I(nc.vector.tensor_tensor, out=xres[xr][:], in0=gen[g][:, :], in1=xres[xr][:], op=ALU.add),
                           reads=[b_gen[g]], writes=[b_xres[xr]])
                    store_toks[xr] = P.dma("pool", I(nc.gpsimd.dma_start, out=out_d[bi, r0:r0 + 128, nh * 512:(nh + 1) * 512],
                                                     in_=xres[xr][:]), f"st{xr}", reads=[b_xres[xr]])

        epsb = sb("epsb", [128, 2], F32)
        b_eps = Buf()
        P.emit("pool", I(nc.gpsimd.memset, epsb[:, 0:1], EPS), writes=[b_eps])
        P.emit("pool", I(nc.gpsimd.memset, epsb[:, 1:2], -0.5), writes=[b_eps])

        for bi in range(2):
            P.mark(3 + 10 * bi)
            on = 0
            P.emit("pool", I(nc.gpsimd.memset, fs[on][:, 0:128], 1.0), writes=[b_fs[on]])
            for nh in range(2):
                for q4 in range(4):
                    ch = nh * 4 + q4
                    dg = 1 + (ch % 7)
                    P.emit("dve", I(nc.vector.tensor_scalar, out=fs[dg][:, 0:128], in0=ident, scalar1=modsb[:, 16 + ch, bi:bi + 1],
                                    scalar2=None, op0=ALU.mult), reads=[b_cbf, b_modsb], writes=[b_fs[dg]])
                    P.emit("pe", I(nc.tensor.matmul, Ob[nh][:, q4 * 128:(q4 + 1) * 128], lhsT=fs[on][:, 0:128], rhs=fs[dg][:, 0:128],
                                   start=True, stop=True), reads=[b_fs[on], b_fs[dg]], writes=[b_O[nh]])
            WST = [2, 3, 6, 7]

            def wout_dma(k):
                for kc in (2 * k, 2 * k + 1):
                    for nh in range(2):
                        st_ = WST[(kc % 2) * 2 + nh]
                        P.dma("sp", I(nc.sync.dma_start, out=fs[st_][:, :], in_=wout_d[kc * 128:(kc + 1) * 128, nh * 512:(nh + 1) * 512]),
                              f"wst{st_}", writes=[b_fs[st_]])

            def wout_mul(k):
                for kc in (2 * k, 2 * k + 1):
                    for nh in range(2):
                        st_ = WST[(kc % 2) * 2 + nh]
                        P.emit("dve", I(nc.vector.tensor_tensor, out=wout[:, kc, nh * 512:(nh + 1) * 512],
                                        in0=Ob[nh][:, :], in1=fs[st_][:, :], op=ALU.mult),
                               reads=[b_O[nh], b_fs[st_]], writes=[b_wout[kc]])

            P.mark(4 + 10 * bi)
            def lat_srcs(g):
                return [x_d[bi, g * 512 + tt * 128: g * 512 + (tt + 1) * 128, :] for tt in range(4)]
            kvg = [(2, 2, 0, 0, False, [ctx_d[bi, tt * 128:(tt + 1) * 128, :] for tt in range(2)])]
            for g in range(4):
                kvg.append((4, bi, 256 + g * 512, 1 + g, True, lat_srcs(g)))
            xphase(kvg[0][5], "mix")
            prev = None
            for i, (nt, j, koff, kidx, rope, srcs) in enumerate(kvg):
                L = i % 2
                nxt = kvg[i + 1][5] if i + 1 < len(kvg) else lat_srcs(0)
                kv_front(nt, j, koff, L, after_t=lambda nxt=nxt: xphase(nxt, "mix"))
                if i >= 1:
                    wout_mul(i - 1)
                if i <= 3:
                    wout_dma(i)
                if rope:
                    load_rope(i - 1)
                kv_back(nt, koff, kidx, rope, L)

            if dbg and bi == 0:
                def dcp(name, src_ap, n, bufs):
                    t = 0
                    while t < n:
                        k = rr_fs.next()
                        P.emit("dve", I(nc.vector.tensor_copy, out=fs[k][:, :], in_=src_ap[:, t:t + 512]), reads=bufs, writes=[b_fs[k]])
                        P.dma("sp", I(nc.sync.dma_start, out=dbg_d[name][:, t:t + 512], in_=fs[k][:, :]), "dbg", reads=[b_fs[k]])
                        t += 512
                dcp("d_KT", KT[:, 0:2048], 2048, b_KT)
                dcp("d_V", Vaug[:].rearrange("p a b c -> p (a b c)"), NKT * 256, b_V)

            P.mark(6 + 10 * bi)
            def side_units(g):
                gp_ = g % 2
                units = []
                for cc in range(4):
                    def u_g(cc=cc):
                        gg_ = proj(1792 + cc * 128, 512, [b_win[3]])
                        sg = 4 + (cc % 2)
                        P.emit("act", I(nc.scalar.activation, out=bs[sg][:, :], in_=gen[gg_][:, :], func=AF.Sigmoid),
                               reads=[b_gen[gg_]], writes=[b_bs[sg]])
                        ga = proj(1280 + cc * 128, 512, [b_win[2]])
                        P.emit("dve", I(nc.vector.tensor_tensor, out=uconv[:, cc, 15 + g * 512: 15 + (g + 1) * 512],
                                        in0=gen[ga][:, :], in1=bs[sg][:, :], op=ALU.mult),
                               reads=[b_gen[ga], b_bs[sg]], writes=[b_uc[cc][g]])
                    units.append(u_g)
                for cc in range(4):
                    def u_zc(cc=cc):
                        gz = proj(2304 + cc * 128, 512, [b_win[3]])
                        P.emit("act", I(nc.scalar.activation, out=zc[gp_][:, cc, :], in_=gen[gz][:, :], func=AF.Silu),
                               reads=[b_gen[gz]], writes=[b_zc[gp_][cc]])
                    units.append(u_zc)
                for c in range(4):
                    def u_za(c=c):
                        gz = proj(768 + c * 128, 512, [b_win[2]])
                        P.emit("act", I(nc.scalar.activation, out=za[gp_][:, c, :], in_=gen[gz][:, :], func=AF.Silu),
                               reads=[b_gen[gz]], writes=[b_za[gp_][c]])
                    units.append(u_za)
                return units

            tphase(4, bi)
            xphase(lat_srcs(1))
            for g in range(4):
                gp = g % 2
                load_rope(g)
                for c in range(4):
                    gq = proj(c * 128, 512, [b_win[1]])
                    qk_A(gq, 512, c)
                for c in range(4):
                    qk_B(512, c)
                if g >= 1:
                    prebuild_diags()
                su = side_units(g)
                for u in su[0:4]:
                    u()
                for c in range(4):
                    qk_C(512, c, vcol(V_QNW), True, [(slice(0, 64), QT[0:64, c, 0, :], b_QT[c]),
                                                     (slice(64, 128), QT[64:128, c, 1, :], b_QT[c])])
                if dbg and bi == 0 and g == 0:
                    dcp("d_QT", QT[:].rearrange("p a b c -> p (a b c)"), 4096, b_QT)
                P.mark(7 + 10 * bi)
                if g >= 1:
                    conv_p1(g - 1)
                for u in su[4:8]:
                    u()
                if g >= 1:
                    conv_p2a()
                for u in su[8:12]:
                    u()
                if g < 3:
                    tphase(4, bi)
                if g >= 1:
                    conv_p2b(1 - gp)
                    if dbg and bi == 0 and g == 1:
                        dcp("d_zc", zc[1 - gp][:].rearrange("p a b -> p (a b)"), 2048, b_zc[1 - gp])
                        dcp("d_za", za[1 - gp][:].rearrange("p a b -> p (a b)"), 2048, b_za[1 - gp])
                    outproj(bi, g - 1, 1 - gp)
                P.mark(8 + 10 * bi)
                hook = (lambda gn=g + 2: xphase(lat_srcs(gn), "dve", "pool")) if g + 2 <= 3 else None
                attention(gp, hook)
                P.mark(6 + 10 * bi)
            P.mark(9 + 10 * bi)
            conv_block(3, 1)
            outproj(bi, 3, 1)

        P.enabled = True
        P.wait_only("sp", [t for t in store_toks.values() if t])
        P.wait_only("pool", [t for t in store_toks.values() if t])
        P.build(nc, st)
    return nc


def _perm_heads():
    idx = []
    for c in range(4):
        for half in range(2):
            h = c + 4 * half
            idx.extend(range(h * 64, (h + 1) * 64))
    return np.array(idx)


def _rope_tables():
    S = S_LAT
    t = np.arange(S)
    row = (t // 64).astype(np.float64)
    col = (t % 64).astype(np.float64)
    freqs = 10000.0 ** (-np.arange(0, 32, 2, dtype=np.float64) / 32.0)
    ang_r = row[:, None] * freqs[None, :]
    ang_c = col[:, None] * freqs[None, :]
    cosT = np.zeros((128, S), np.float32)
    sinT = np.zeros((128, S), np.float32)
    for p in range(128):
        d = p % 64
        if d < 32:
            a = ang_r[:, d % 16]
        else:
            a = ang_c[:, (d - 32) % 16]
        cosT[p] = np.cos(a)
        sinT[p] = np.sin(a)
    return cosT, sinT


def _consts():
    c = np.zeros((128, 512), np.float32)
    c[:, 0:128] = np.eye(128, dtype=np.float32)
    R = np.zeros((128, 128), np.float32)
    for m in range(128):
        if (m % 32) < 16:
            R[m + 16, m] = -1.0
        else:
            R[m - 16, m] = 1.0
    c[:, 128:256] = R
    bo = np.zeros((128, 128), np.float32)
    bo[0:64, 0:64] = 1.0 / 64.0
    bo[64:128, 64:128] = 1.0 / 64.0
    c[:, 256:384] = bo
    c[:, 384:512] = 1.0 / 512.0
    sel = np.zeros((3, 256), np.float32)
    sel[0, 0:128] = 1.0
    sel[1, 128:256] = 1.0
    return c, sel


_NC_CACHE = {}


def _prep(inputs):
    f = lambda a: np.ascontiguousarray(np.asarray(a, dtype=np.float32))
    x = f(inputs["x"]); c = f(inputs["c"]); ctx = f(inputs["ctx"]); c_ctx = f(inputs["c_ctx"])
    w_mod = f(inputs["w_mod"])[0]; b_mod = f(inputs["b_mod"])[0]; norm_w = f(inputs["norm_w"])[0]
    w_in = f(inputs["w_in"])[0]; qn = f(inputs["q_norm_w"])[0]; kn = f(inputs["k_norm_w"])[0]
    conv_w = f(inputs["conv_w"])[0]; conv_b = f(inputs["conv_b"])[0]
    ln_w = f(inputs["conv_ln_w"])[0]; ln_b = f(inputs["conv_ln_b"])[0]
    w_pw = f(inputs["w_pw"])[0]; b_pw = f(inputs["b_pw"])[0]; w_out = f(inputs["w_out"])[0]
    perm = _perm_heads()
    cols = np.arange(2816)
    cols[0:512] = perm
    cols[768:1280] = 768 + perm
    w_in_p = np.ascontiguousarray(w_in[:, cols])
    rows = np.arange(1024)
    rows[0:512] = perm
    w_out_p = np.ascontiguousarray(w_out[rows, :])
    vecs = np.zeros((128, NV), np.float32)
    vecs[:, V_NORMW:V_NORMW + 8] = norm_w.reshape(8, 128).T
    vecs[:, V_BMOD:V_BMOD + 24] = b_mod.reshape(24, 128).T
    vecs[:, V_QNW] = np.tile(qn, 2)
    vecs[:, V_KNW] = np.tile(kn, 2)
    vecs[:, V_CB:V_CB + 4] = conv_b.reshape(4, 128).T
    vecs[:, V_LNW:V_LNW + 4] = ln_w.reshape(4, 128).T
    vecs[:, V_LNB:V_LNB + 4] = ln_b.reshape(4, 128).T
    vecs[:, V_BPW:V_BPW + 4] = b_pw.reshape(4, 128).T
    vecs[:, V_CW:V_CW + 124] = conv_w.reshape(31, 4, 128).transpose(2, 1, 0).reshape(128, 124)
    constsf = np.concatenate([np.eye(128, dtype=np.float32), np.ones((128, 128), np.float32)], axis=1)
    consts, sel = _consts()
    cosT, sinT = _rope_tables()
    in_maps = []
    for i in range(8):
        cc = np.stack([c[2 * i], c[2 * i + 1], c_ctx], axis=1)
        cT = np.ascontiguousarray(cc.reshape(8, 128, 3).transpose(1, 0, 2).reshape(128, 24))
        in_maps.append({
            "x": np.ascontiguousarray(x[2 * i:2 * i + 2]), "ctx": np.ascontiguousarray(ctx[2 * i:2 * i + 2]),
            "cT": cT, "w_mod": w_mod, "w_in": w_in_p, "w_pw": w_pw, "w_out": w_out_p, "vecs": vecs,
            "consts": consts, "cosT": cosT, "sinT": sinT,
        })
    return in_maps


def kernel(**inputs):
    in_maps = _prep(inputs)
    if "nc" not in _NC_CACHE:
        _NC_CACHE["nc"] = build_nc()
    nc = _NC_CACHE["nc"]
    res = run_bass_kernel_spmd(nc, in_maps, core_ids=list(range(8)))
    return np.concatenate([np.asarray(r["out"], dtype=np.float32) for r in res.results], axis=0)
```

```python
import contextlib
import numpy as np
import concourse.bass as bass
import concourse.mybir as mybir
from concourse.bass_utils import run_bass_kernel_spmd

F32 = mybir.dt.float32
BF16 = mybir.dt.bfloat16
AF = mybir.ActivationFunctionType
ALU = mybir.AluOpType

ENGS = ["pe", "act", "dve", "pool", "sp"]
import os
SAME_ENGINE_SYNC = not os.environ.get("NOSES")
EPS = 1e-6
S_LAT = 2048
C_LEN = 256
NKT = (S_LAT + C_LEN) // 128

V_NORMW = 0
V_BMOD = 8
V_QNW = 32
V_KNW = 33
V_CB = 34
V_LNW = 38
V_LNB = 42
V_BPW = 46
V_CW = 50
NV = 50 + 124


class Buf:
    __slots__ = ("name", "w", "r")

    def __init__(self, name=""):
        self.name = name
        self.w = None
        self.r = []


def I(fn, *a, **k):
    return lambda: fn(*a, **k)


class Prog:
    def __init__(self):
        self.ops = {e: [] for e in ENGS}
        self.cnt = {}
        self.waited = {e: {} for e in ENGS}
        self.semkeys = []
        self.glob = []
        self.enabled = True
        self.stage = 0
        self.stop_stage = 10 ** 9

    def sub(self, x):
        if self.enabled and self.base + x > self.stop_stage:
            self.enabled = False

    def mark(self, n):
        self.base = n
        self.stage = n
        if n > self.stop_stage:
            self.enabled = False

    def _sem(self, key):
        if key not in self.cnt:
            self.cnt[key] = 0
            self.semkeys.append(key)

    def _waits(self, eng, reads, writes, extra):
        deps = {}

        def add(tok):
            if tok is None:
                return
            k, v = tok
            if deps.get(k, 0) < v:
                deps[k] = v
        for b in reads:
            add(b.w)
        for b in writes:
            add(b.w)
            for t in b.r:
                add(t)
        for t in extra:
            add(t)
        out = []
        for k, v in deps.items():
            if k == eng and (not SAME_ENGINE_SYNC or eng == "pe"):
                continue
            if self.waited[eng].get(k, 0) >= v:
                continue
            self.waited[eng][k] = v
            out.append((k, v))
        return out

    def emit(self, eng, fns, reads=(), writes=(), extra=()):
        if not self.enabled:
            return None
        if callable(fns):
            fns = [fns]
        self._sem(eng)
        waits = self._waits(eng, reads, writes, extra)
        self.cnt[eng] += 1
        tok = (eng, self.cnt[eng])
        self.glob.append((eng, (waits, list(fns), (eng, 1))))
        for b in reads:
            b.r.append(tok)
        for b in writes:
            b.w = tok
            b.r = []
        return tok

    def dma(self, queue, fn, semkey, reads=(), writes=(), extra=()):
        if not self.enabled:
            return None
        self._sem(semkey)
        waits = self._waits(queue, reads, writes, extra)
        self.cnt[semkey] += 16
        tok = (semkey, self.cnt[semkey])
        self.glob.append((queue, (waits, [fn], (semkey, 16))))
        for b in reads:
            b.r.append(tok)
        for b in writes:
            b.w = tok
            b.r = []
        return tok

    def wait_only(self, eng, toks):
        waits = self._waits(eng, (), (), toks)
        if waits:
            self.glob.append((eng, (waits, [], None)))

    def build(self, nc, stack, max_ins=10 ** 9):
        sems = {}
        for k in self.semkeys:
            sems[k] = stack.enter_context(nc.semaphore("s_" + k))
        chunks = []
        cur = {e: [] for e in ENGS}
        cnt = {e: 0 for e in ENGS}
        for eng, op in self.glob:
            n = len(op[0]) + len(op[1])
            if cnt[eng] + n > max_ins and cnt[eng] > 0:
                chunks.append(cur)
                cur = {e: [] for e in ENGS}
                cnt = {e: 0 for e in ENGS}
            cur[eng].append(op)
            cnt[eng] += n
        chunks.append(cur)
        self.nchunks = len(chunks)

        def make(ops):
            def body(engine):
                for waits, fns, inc in ops:
                    for k, v in waits:
                        engine.wait_ge(sems[k], v)
                    for i, fn in enumerate(fns):
                        ins = fn()
                        if i == len(fns) - 1 and inc is not None:
                            ins.then_inc(sems[inc[0]], inc[1])
            return body

        for ch in chunks:
            with nc.Block() as block:
                engdec = {"pe": block.tensor, "act": block.scalar, "dve": block.vector,
                          "pool": block.gpsimd, "sp": block.sync}
                for e in ENGS:
                    if ch[e]:
                        engdec[e](make(ch[e]))


class RR:
    def __init__(self, n):
        self.n = n
        self.i = -1

    def next(self):
        self.i = (self.i + 1) % self.n
        return self.i


def build_nc(dbg=False, stop_stage=10 ** 9):
    nc = bass.Bass("TRN2", target_bir_lowering=False)
    x_d = nc.dram_tensor("x", [2, S_LAT, 1024], F32, kind="ExternalInput").ap()
    ctx_d = nc.dram_tensor("ctx", [2, C_LEN, 1024], F32, kind="ExternalInput").ap()
    cT_d = nc.dram_tensor("cT", [128, 24], F32, kind="ExternalInput").ap()
    wmod_d = nc.dram_tensor("w_mod", [1024, 3072], F32, kind="ExternalInput").ap()
    win_d = nc.dram_tensor("w_in", [1024, 2816], F32, kind="ExternalInput").ap()
    wpw_d = nc.dram_tensor("w_pw", [512, 512], F32, kind="ExternalInput").ap()
    wout_d = nc.dram_tensor("w_out", [1024, 1024], F32, kind="ExternalInput").ap()
    vecs_d = nc.dram_tensor("vecs", [128, NV], F32, kind="ExternalInput").ap()
    consts_d = nc.dram_tensor("consts", [128, 512], F32, kind="ExternalInput").ap()
    cos_d = nc.dram_tensor("cosT", [128, S_LAT], F32, kind="ExternalInput").ap()
    sin_d = nc.dram_tensor("sinT", [128, S_LAT], F32, kind="ExternalInput").ap()
    out_d = nc.dram_tensor("out", [2, S_LAT, 1024], F32, kind="ExternalOutput").ap()
    dbg_d = {}
    if dbg:
        for name, shp in [("d_uT", [128, 8 * 512]), ("d_KT", [128, 2304]), ("d_V", [128, NKT * 256]),
                          ("d_QT", [128, 8 * 512]), ("d_za", [128, 4 * 512]), ("d_zc", [128, 4 * 512]),
                          ("d_uc", [128, 4 * 2078]), ("d_AB", [128, 48])]:
            dbg_d[name] = nc.dram_tensor(name, shp, F32, kind="ExternalOutput").ap()

    P = Prog()
    P.stop_stage = stop_stage
    st = contextlib.ExitStack()
    with st:
        def sb(name, shape, dt):
            return st.enter_context(nc.sbuf_tensor("sb_" + name, shape, dt))

        def ps(name, shape, dt):
            return st.enter_context(nc.psum_tensor("ps_" + name, shape, dt))

        win = sb("win", [128, 8, 2816], BF16)
        b_win = [Buf() for _ in range(4)]
        wout = sb("wout", [128, 8, 1024], BF16)
        b_wout = [Buf() for _ in range(8)]
        wpw = sb("wpw", [128, 4, 512], BF16)
        b_wpw = Buf()
        diag = [sb(f"diag{i}", [128, 31, 128], BF16) for i in range(2)]
        b_diag = [Buf() for _ in range(2)]
        rr_diag = RR(2)
        uT = sb("uT", [128, 8, 512], BF16)
        b_uT = [Buf() for _ in range(8)]
        xn = sb("xn", [128, 4, 1024], BF16)
        b_xn = [Buf() for _ in range(4)]
        xt = [sb(f"xt{i}", [128, 1024], F32) for i in range(3)]
        b_xt = [Buf() for _ in range(3)]
        rr_xt = RR(3)
        QT = sb("QT", [128, 4, 2, 512], BF16)
        b_QT = [Buf() for _ in range(4)]
        KT = sb("KT", [128, 2304], BF16)
        b_KT = [Buf() for _ in range(5)]
        Vaug = sb("Vaug", [128, NKT, 2, 128], BF16)
        b_V = [Buf() for _ in range(NKT)]
        za = [sb(f"za{i}", [128, 4, 512], BF16) for i in range(2)]
        b_za = [[Buf() for _ in range(4)] for _ in range(2)]
        zc = [sb(f"zc{i}", [128, 4, 512], BF16) for i in range(2)]
        b_zc = [[Buf() for _ in range(4)] for _ in range(2)]
        uconv = sb("uconv", [128, 4, 2078], BF16)
        b_uc = [[Buf() for _ in range(4)] for _ in range(4)]
        b_ucpad = Buf()
        NPT = 3
        PT = [sb(f"PT{i}", [128, 1024], BF16) for i in range(NPT)]
        b_PT = [Buf() for _ in range(NPT)]
        rr_PT = RR(NPT)
        NF = 8
        fs = [sb(f"fs{i}", [128, 512], F32) for i in range(NF)]
        b_fs = [Buf() for _ in range(NF)]
        rr_fs = RR(NF)
        NB = 6
        bs = [sb(f"bs{i}", [128, 512], BF16) for i in range(NB)]
        b_bs = [Buf() for _ in range(NB)]
        rr_bs = RR(NB)
        ybf = sb("ybf", [128, 4, 512], BF16)
        b_ybf = [Buf() for _ in range(4)]
        y2 = sb("y2", [128, 4, 512], BF16)
        y2f = y2[:].rearrange("p a b -> p (a b)")
        b_y2 = [Buf() for _ in range(4)]
        xres = [sb(f"xres{i}", [128, 512], F32) for i in range(2)]
        b_xres = [Buf() for _ in range(2)]
        rr_xres = RR(2)
        cosb, sinb = xres[0], xres[1]
        b_cos, b_sin = b_xres[0], b_xres[1]
        cbf = sb("cbf", [128, 512], BF16)
        b_cbf = Buf()
        ident = cbf[:, 0:128]
        Rm = cbf[:, 128:256]
        bones = cbf[:, 256:384]
        odiv = cbf[:, 384:512]
        vecs = sb("vecs", [128, NV], F32)
        b_vecs = Buf()
        cT = sb("cT", [128, 24], F32)
        b_cT = Buf()
        scT = sb("scT", [128, 24], BF16)
        b_scT = Buf()
        modsb = sb("modsb", [128, 24, 3], F32)
        b_modsb = Buf()
        Asb = sb("Asb", [128, 8, 3], F32)
        b_A = Buf()
        NST = 8
        stat = sb("stat", [128, NST, 4], F32)
        b_stat = [Buf() for _ in range(NST)]
        rr_stat = RR(NST)

        NG = 2
        gen = [ps(f"gen{i}", [128, 512], F32) for i in range(NG)]
        b_gen = [Buf() for _ in range(NG)]
        rr_gen = RR(NG)
        Sb = [ps(f"S{i}", [128, 1024], F32) for i in range(2)]
        b_S = [Buf() for _ in range(2)]
        rr_S = RR(2)
        Tbs = [Sb[0][:, 0:512].bitcast(BF16), Sb[1][:, 0:512].bitcast(BF16)]
        b_T = b_S
        rr_T = RR(2)
        Ob = [ps(f"O{i}", [128, 512], F32) for i in range(2)]
        b_O = [Buf() for _ in range(2)]
        rr_O = RR(2)

        def vcol(off, n=1):
            return vecs[:, off:off + n]

        P.dma("sp", I(nc.sync.dma_start, out=vecs[:], in_=vecs_d[:, :]), "c_vecs", writes=[b_vecs])
        P.dma("sp", I(nc.sync.dma_start, out=cT[:], in_=cT_d[:, :]), "c_cT", writes=[b_cT])
        P.dma("pool", I(nc.gpsimd.dma_start, out=cbf[:], in_=consts_d[:, :]), "c_cbf", writes=[b_cbf])
        win_v = win_d.rearrange("(kc p) n -> p kc n", p=128)
        pieces = [(512, 768), (0, 512), (768, 1792), (1792, 2816)]
        def load_w(i):
            lo, hi = pieces[i]
            P.dma("pool", I(nc.gpsimd.dma_start, out=win[:, :, lo:hi], in_=win_v[:, :, lo:hi]), f"w_in{i}",
                  writes=[b_win[i]], extra=wmod_toks)
        wmod_toks = []
        load_w(0)
        P.emit("pool", I(nc.gpsimd.memset, uconv[:, :, 0:15], 0.0), writes=[b_ucpad])
        P.emit("pool", I(nc.gpsimd.memset, uconv[:, :, 2063:2078], 0.0), writes=[b_ucpad])
        P.emit("pool", I(nc.gpsimd.memset, Vaug[:], 1.0), writes=b_V)
        P.emit("pool", I(nc.gpsimd.memset, QT[:], 0.0), writes=b_QT)

        P.mark(1)
        P.emit("act", I(nc.scalar.activation, out=scT[:], in_=cT[:], func=AF.Silu), reads=[b_cT], writes=[b_scT])
        scT3 = scT[:].rearrange("p (k j) -> p k j", j=3)
        modps = gen[0]
        first = True
        wmod_last = {}
        for kc in range(8):
            for third in range(3):
                s = rr_xt.next()
                xtb = xt[s][:].bitcast(BF16)[:, 0:1024]
                tk = P.dma("pool", I(nc.gpsimd.dma_start, out=xtb, in_=wmod_d[kc * 128:(kc + 1) * 128, third * 1024:(third + 1) * 1024]),
                           f"wm{s}", writes=[b_xt[s]])
                wmod_last[s] = tk
                fns = []
                for jj in range(8):
                    col = (third * 8 + jj) * 3
                    fns.append(I(nc.tensor.matmul, modps[:, col:col + 3], lhsT=xtb[:, jj * 128:(jj + 1) * 128],
                                 rhs=scT3[:, kc, :], start=first, stop=(kc == 7 and third == 2 and jj == 7),
                                 skip_group_check=True))
                    first = False
                P.emit("pe", fns, reads=[b_xt[s], b_scT], writes=[b_gen[0]])
        wmod_toks = list(wmod_last.values())
        for i in (1, 2, 3):
            load_w(i)
        P.dma("pool", I(nc.gpsimd.dma_start, out=wpw[:], in_=wpw_d.rearrange("(cc p) n -> p cc n", p=128)),
              "w_pw", writes=[b_wpw])
        P.mark(2)
        P.emit("dve", I(nc.vector.tensor_tensor, out=modsb[:], in0=modps[:, 0:72].rearrange("p (c j) -> p c j", j=3),
                        in1=vcol(V_BMOD, 24).unsqueeze(2).to_broadcast([128, 24, 3]), op=ALU.add),
               reads=[b_gen[0], b_vecs], writes=[b_modsb])
        P.emit("dve", I(nc.vector.scalar_tensor_tensor, out=Asb[:], in0=modsb[:, 8:16, :], scalar=1.0,
                        in1=vcol(V_NORMW, 8).unsqueeze(2).to_broadcast([128, 8, 3]), op0=ALU.add, op1=ALU.mult),
               reads=[b_modsb, b_vecs], writes=[b_A])
        def Acol(kc, j):
            return Asb[:, kc, j:j + 1]

        def Bcol(kc, j):
            return modsb[:, kc, j:j + 1]

        def xphase(srcs, sq_eng="dve", rstd_eng="act"):
            for tt, src in enumerate(srcs):
                s = rr_xt.next()
                P.dma("sp", I(nc.sync.dma_start, out=xt[s][:], in_=src), f"xt{s}", writes=[b_xt[s]])
                q = rr_stat.next()
                if sq_eng == "act" or (sq_eng == "mix" and tt % 2 == 0):
                    P.emit("act", I(nc.scalar.activation, out=y2f[:, 0:1024], in_=xt[s][:], func=AF.Square,
                                    accum_out=stat[:, q, 0:1]), reads=[b_xt[s]], writes=b_y2[0:2] + [b_stat[q]])
                else:
                    P.emit("dve", I(nc.vector.scalar_tensor_tensor, out=y2f[:, 0:1024], in0=xt[s][:], scalar=1.0, in1=xt[s][:],
                                    op0=ALU.mult, op1=ALU.mult, accum_out=stat[:, q, 0:1]),
                           reads=[b_xt[s]], writes=b_y2[0:2] + [b_stat[q]])
                if rstd_eng == "pool":
                    P.emit("pool", I(nc.gpsimd.tensor_scalar, out=stat[:, q, 1:2], in0=stat[:, q, 0:1], scalar1=1.0 / 1024.0,
                                     scalar2=EPS, op0=ALU.mult, op1=ALU.add), reads=[b_stat[q]], writes=[b_stat[q]])
                    P.emit("pool", I(nc.gpsimd.tensor_tensor, out=stat[:, q, 2:3], in0=stat[:, q, 1:2], in1=epsb[:, 1:2], op=ALU.pow),
                           reads=[b_stat[q], b_eps], writes=[b_stat[q]])
                else:
                    P.emit("act", I(nc.scalar.activation, out=stat[:, q, 1:2], in_=stat[:, q, 0:1], func=AF.Ln,
                                    scale=1.0 / 1024.0, bias=epsb[:, 0:1]), reads=[b_stat[q], b_eps], writes=[b_stat[q]])
                    P.emit("act", I(nc.scalar.activation, out=stat[:, q, 2:3], in_=stat[:, q, 1:2], func=AF.Exp,
                                    scale=-0.5), reads=[b_stat[q]], writes=[b_stat[q]])
                P.emit("pool", I(nc.gpsimd.tensor_scalar, out=xn[:, tt, :], in0=xt[s][:], scalar1=stat[:, q, 2:3],
                                 scalar2=0.0, op0=ALU.mult, op1=ALU.add),
                       reads=[b_xt[s], b_stat[q]], writes=[b_xn[tt]])

        def tphase(nt, j):
            for kc in range(8):
                g = rr_T.next()
                tb = Tbs[g]
                fns = [I(nc.tensor.transpose, out=tb[:, tt * 128:(tt + 1) * 128], in_=xn[:, tt, kc * 128:(kc + 1) * 128],
                         identity=ident) for tt in range(nt)]
                P.emit("pe", fns, reads=b_xn[:nt] + [b_cbf], writes=[b_T[g]])
                P.emit("dve", I(nc.vector.tensor_scalar, out=uT[:, kc, 0:nt * 128], in0=tb[:, 0:nt * 128],
                                scalar1=Acol(kc, j), scalar2=Bcol(kc, j), op0=ALU.mult, op1=ALU.add),
                       reads=[b_T[g], b_A, b_modsb], writes=[b_uT[kc]])

        def proj(lo, N, wbufs):
            g = rr_gen.next()
            fns = [I(nc.tensor.matmul, gen[g][:, 0:N], lhsT=win[:, kc, lo:lo + 128], rhs=uT[:, kc, 0:N],
                     start=(kc == 0), stop=(kc == 7)) for kc in range(8)]
            P.emit("pe", fns, reads=b_uT + wbufs, writes=[b_gen[g]])
            return g

        def qk_A(g, N, L):
            P.emit("dve", I(nc.vector.tensor_copy, out=fs[L][:, 0:N], in_=gen[g][:, 0:N]),
                   reads=[b_gen[g]], writes=[b_fs[L]])
            P.emit("pool", I(nc.gpsimd.tensor_tensor, out=bs[L][:, 0:N], in0=fs[L][:, 0:N], in1=fs[L][:, 0:N], op=ALU.mult),
                   reads=[b_fs[L]], writes=[b_bs[L]])

        def qk_B(N, L):
            m = rr_gen.next()
            P.emit("pe", I(nc.tensor.matmul, gen[m][:, 0:N], lhsT=bones, rhs=bs[L][:, 0:N], start=True, stop=True),
                   reads=[b_bs[L], b_cbf], writes=[b_gen[m]])
            P.emit("act", I(nc.scalar.activation, out=fs[4 + L][:, 0:N], in_=gen[m][:, 0:N], func=AF.Ln, bias=epsb[:, 0:1]),
                   reads=[b_gen[m], b_eps], writes=[b_fs[4 + L]])
            P.emit("act", I(nc.scalar.activation, out=fs[4 + L][:, 0:N], in_=fs[4 + L][:, 0:N], func=AF.Exp, scale=-0.5),
                   reads=[b_fs[4 + L]], writes=[b_fs[4 + L]])

        def qk_C(N, L, wcol, rope, dests):
            raw, r = L, 4 + L
            if not rope:
                for rows, dest, dbuf in dests:
                    P.emit("dve", I(nc.vector.scalar_tensor_tensor, out=dest, in0=fs[raw][rows, 0:N], scalar=wcol[rows, :],
                                    in1=fs[r][rows, 0:N], op0=ALU.mult, op1=ALU.mult),
                           reads=[b_fs[raw], b_fs[r], b_vecs], writes=[dbuf])
                return
            P.emit("dve", I(nc.vector.scalar_tensor_tensor, out=bs[L][:, 0:N], in0=fs[raw][:, 0:N], scalar=wcol,
                            in1=fs[r][:, 0:N], op0=ALU.mult, op1=ALU.mult),
                   reads=[b_fs[raw], b_fs[r], b_vecs], writes=[b_bs[L]])
            m = rr_gen.next()
            P.emit("pe", I(nc.tensor.matmul, gen[m][:, 0:N], lhsT=Rm, rhs=bs[L][:, 0:N], start=True, stop=True),
                   reads=[b_bs[L], b_cbf], writes=[b_gen[m]])
            P.emit("pool", I(nc.gpsimd.tensor_tensor, out=fs[raw][:, 0:N], in0=bs[L][:, 0:N], in1=cosb[:, 0:N], op=ALU.mult),
                   reads=[b_bs[L], b_cos], writes=[b_fs[raw]])
            P.emit("dve", I(nc.vector.tensor_tensor, out=fs[r][:, 0:N], in0=gen[m][:, 0:N], in1=sinb[:, 0:N], op=ALU.mult),
                   reads=[b_gen[m], b_sin], writes=[b_fs[r]])
            for rows, dest, dbuf in dests:
                P.emit("pool", I(nc.gpsimd.tensor_tensor, out=dest, in0=fs[raw][rows, 0:N], in1=fs[r][rows, 0:N], op=ALU.add),
                       reads=[b_fs[raw], b_fs[r]], writes=[dbuf])

        def load_rope(g):
            P.dma("sp", I(nc.sync.dma_start, out=cosb[:], in_=cos_d[:, g * 512:(g + 1) * 512]), "cosb", writes=[b_cos])
            P.dma("sp", I(nc.sync.dma_start, out=sinb[:], in_=sin_d[:, g * 512:(g + 1) * 512]), "sinb", writes=[b_sin])

        def kv_front(nt, j, koff, L, after_t=None):
            N = nt * 128
            tphase(nt, j)
            if after_t is not None:
                after_t()
            g = proj(512, N, [b_win[0]])
            qk_A(g, N, L)
            qk_B(N, L)
            for tt in range(nt):
                gv = rr_gen.next()
                fns = [I(nc.tensor.matmul, gen[gv][:, 0:128], lhsT=uT[:, kc, tt * 128:(tt + 1) * 128],
                         rhs=win[:, kc, 640:768], start=(kc == 0), stop=(kc == 7)) for kc in range(8)]
                P.emit("pe", fns, reads=b_uT + [b_win[0]], writes=[b_gen[gv]])
                tile = koff // 128 + tt
                P.emit("act", I(nc.scalar.copy, out=Vaug[:, tile, 0, 0:64], in_=gen[gv][:, 0:64]),
                       reads=[b_gen[gv]], writes=[b_V[tile]])
                P.emit("act", I(nc.scalar.copy, out=Vaug[:, tile, 1, 64:128], in_=gen[gv][:, 64:128]),
                       reads=[b_gen[gv]], writes=[b_V[tile]])

        def kv_back(nt, koff, kidx, rope, L):
            N = nt * 128
            qk_C(N, L, vcol(V_KNW), rope, [(slice(0, 128), KT[:, koff:koff + N], b_KT[kidx])])

        def attention(gp, hook=None, bg=()):
            NP = NKT // 2
            items = [(c, half, pr) for c in range(4) for half in range(2) for pr in range(NP)]
            sbank = {}

            def emitS(idx):
                c, half, pr = items[idx]
                s = rr_S.next()
                sbank[idx] = s
                fns = []
                kset = set()
                for e in range(2):
                    kt = 2 * pr + e
                    kset.add(0 if kt < 2 else 1 + (kt - 2) // 4)
                    fns.append(I(nc.tensor.matmul, Sb[s][:, e * 512:(e + 1) * 512], lhsT=KT[:, kt * 128:(kt + 1) * 128],
                                 rhs=QT[:, c, half, :], start=True, stop=True))
                P.emit("pe", fns, reads=[b_KT[k] for k in kset] + [b_QT[c]], writes=[b_S[s]])
            SFIRST = not os.environ.get("NOSFIRST")
            emitS(0)
            if SFIRST:
                emitS(1)
            o = None
            bg = list(bg)
            bg_every = max(1, (len(items) - 8) // max(1, len(bg))) if bg else 0
            for idx, (c, half, pr) in enumerate(items):
                if idx == 8 and hook is not None:
                    hook()
                if bg and idx >= 4 and (idx - 4) % bg_every == 0:
                    bg.pop(0)()
                if not SFIRST and idx + 1 < len(items):
                    emitS(idx + 1)
                s = sbank.pop(idx)
                p = rr_PT.next()
                P.emit("act", I(nc.scalar.activation, out=PT[p][:, :], in_=Sb[s][:, :], func=AF.Exp, scale=0.125),
                       reads=[b_S[s]], writes=[b_PT[p]])
                if pr == 0:
                    o = rr_O.next()
                fns = []
                for e in range(2):
                    kt = 2 * pr + e
                    fns.append(I(nc.tensor.matmul, Ob[o][:, :], lhsT=Vaug[:, kt, half, :], rhs=PT[p][:, e * 512:(e + 1) * 512],
                                 start=(kt == 0), stop=(kt == NKT - 1)))
                if SFIRST and idx + 2 < len(items):
                    emitS(idx + 2)
                P.emit("pe", fns, reads=[b_V[2 * pr], b_V[2 * pr + 1], b_PT[p]], writes=[b_O[o]])
                if pr == NP - 1:
                    orow = slice(half * 64, half * 64 + 64)
                    srow = slice((1 - half) * 64, (1 - half) * 64 + 64)
                    ri = rr_fs.next()
                    P.emit("dve", I(nc.vector.reciprocal, out=fs[ri][srow, :], in_=Ob[o][srow, :]),
                           reads=[b_O[o]], writes=[b_fs[ri]])
                    ot = rr_fs.next()
                    P.emit("dve", I(nc.vector.tensor_tensor, out=fs[ot][orow, :], in0=Ob[o][orow, :], in1=fs[ri][srow, :],
                                    op=ALU.mult), reads=[b_O[o], b_fs[ri]], writes=[b_fs[ot]])
                    P.emit("pool", I(nc.gpsimd.tensor_tensor, out=za[gp][orow, c, :], in0=fs[ot][orow, :],
                                     in1=za[gp][orow, c, :], op=ALU.mult), reads=[b_fs[ot]], writes=[b_za[gp][c]])
            while bg:
                bg.pop(0)()

        cst = {}

        def build_diag(cc):
            d = rr_diag.next()
            fns = [I(nc.vector.tensor_scalar, out=diag[d][:, j, :], in0=ident, scalar1=vcol(V_CW + cc * 31 + j),
                     scalar2=None, op0=ALU.mult) for j in range(31)]
            P.emit("dve", fns, reads=[b_cbf, b_vecs], writes=[b_diag[d]])
            return d

        def prebuild_diags():
            cst["prebuilt"] = {0: build_diag(0), 1: build_diag(1)}

        def conv_block(gg, gp):
            conv_p1(gg)
            conv_p2a()
            conv_p2b(gp)

        def conv_p1(gg):
            mu_ps, msq_ps = Ob[0], Ob[1]
            for cc in range(4):
                if cc in cst.get("prebuilt", {}):
                    d = cst["prebuilt"].pop(cc)
                else:
                    d = build_diag(cc)
                g = rr_gen.next()
                fns = [I(nc.tensor.matmul, gen[g][:, :], lhsT=diag[d][:, j, :],
                         rhs=uconv[:, cc, gg * 512 + j: gg * 512 + j + 512], start=(j == 0), stop=(j == 30)) for j in range(31)]
                ucr = [b_uc[cc][k] for k in range(max(0, gg - 1), min(3, gg + 1) + 1)] + [b_ucpad]
                P.emit("pe", fns, reads=[b_diag[d]] + ucr, writes=[b_gen[g]])
                P.emit("act", I(nc.scalar.activation, out=ybf[:, cc, :], in_=gen[g][:, :], func=AF.Identity,
                                bias=vcol(V_CB + cc)), reads=[b_gen[g], b_vecs], writes=[b_ybf[cc]])
                sq = rr_bs.next()
                P.emit("pool", I(nc.gpsimd.tensor_tensor, out=bs[sq][:, :], in0=ybf[:, cc, :], in1=ybf[:, cc, :], op=ALU.mult),
                       reads=[b_ybf[cc]], writes=[b_bs[sq]])
                P.emit("pe", I(nc.tensor.matmul, mu_ps[:, :], lhsT=odiv, rhs=ybf[:, cc, :], start=(cc == 0), stop=(cc == 3)),
                       reads=[b_ybf[cc], b_cbf], writes=[b_O[0]])
                P.emit("pe", I(nc.tensor.matmul, msq_ps[:, :], lhsT=odiv, rhs=bs[sq][:, :], start=(cc == 0), stop=(cc == 3)),
                       reads=[b_bs[sq], b_cbf], writes=[b_O[1]])

        def conv_p2a():
            mu_ps, msq_ps = Ob[0], Ob[1]
            mu = rr_fs.next()
            P.emit("act", I(nc.scalar.copy, out=fs[mu][:, :], in_=mu_ps[:, :]), reads=[b_O[0]], writes=[b_fs[mu]])
            m2 = rr_fs.next()
            cst["mu"], cst["m2"] = mu, m2
            P.emit("pool", I(nc.gpsimd.tensor_tensor, out=fs[m2][:, :], in0=fs[mu][:, :], in1=fs[mu][:, :], op=ALU.mult),
                   reads=[b_fs[mu]], writes=[b_fs[m2]])
            P.emit("dve", I(nc.vector.tensor_tensor, out=fs[m2][:, :], in0=msq_ps[:, :], in1=fs[m2][:, :], op=ALU.subtract),
                   reads=[b_O[1], b_fs[m2]], writes=[b_fs[m2]])
            P.emit("act", I(nc.scalar.activation, out=fs[m2][:, :], in_=fs[m2][:, :], func=AF.Ln, bias=epsb[:, 0:1]),
                   reads=[b_fs[m2], b_eps], writes=[b_fs[m2]])
            P.emit("act", I(nc.scalar.activation, out=fs[m2][:, :], in_=fs[m2][:, :], func=AF.Exp, scale=-0.5),
                   reads=[b_fs[m2]], writes=[b_fs[m2]])

        def conv_p2b(gp):
            mu, m2 = cst["mu"], cst["m2"]
            g0, g1 = rr_gen.next(), rr_gen.next()
            banks = [(gen[g0], b_gen[g0]), (gen[g1], b_gen[g1]), (Ob[0], b_O[0]), (Ob[1], b_O[1])]
            for cc in range(4):
                t = rr_fs.next()
                P.emit("dve", I(nc.vector.tensor_tensor, out=fs[t][:, :], in0=ybf[:, cc, :], in1=fs[mu][:, :], op=ALU.subtract),
                       reads=[b_ybf[cc], b_fs[mu]], writes=[b_fs[t]])
                P.emit("dve", I(nc.vector.tensor_tensor, out=fs[t][:, :], in0=fs[t][:, :], in1=fs[m2][:, :], op=ALU.mult),
                       reads=[b_fs[t], b_fs[m2]], writes=[b_fs[t]])
                P.emit("act", I(nc.scalar.activation, out=y2[:, cc, :], in_=fs[t][:, :], func=AF.Silu,
                                scale=vcol(V_LNW + cc), bias=vcol(V_LNB + cc)), reads=[b_fs[t], b_vecs], writes=[b_y2[cc]])
                fns = [I(nc.tensor.matmul, banks[nn][0][:, :], lhsT=wpw[:, cc, nn * 128:(nn + 1) * 128], rhs=y2[:, cc, :],
                         start=(cc == 0), stop=(cc == 3)) for nn in range(4)]
                P.emit("pe", fns, reads=[b_y2[cc], b_wpw], writes=[bk[1] for bk in banks])
            for nn in range(4):
                P.emit("dve", I(nc.vector.scalar_tensor_tensor, out=zc[gp][:, nn, :], in0=banks[nn][0][:, :], scalar=vcol(V_BPW + nn),
                                in1=zc[gp][:, nn, :], op0=ALU.add, op1=ALU.mult),
                       reads=[banks[nn][1], b_vecs], writes=[b_zc[gp][nn]])

        store_toks = {}

        def outproj(bi, gg, gp):
            for tt in range(4):
                r0 = gg * 512 + tt * 128
                for nh in range(2):
                    xr = rr_xres.next()
                    P.dma("sp", I(nc.sync.dma_start, out=xres[xr][:], in_=x_d[bi, r0:r0 + 128, nh * 512:(nh + 1) * 512]),
                          f"xr{xr}", writes=[b_xres[xr]])
                    g = rr_gen.next()
                    fns = []
                    for dch in range(8):
                        src = za[gp][:, dch, tt * 128:(tt + 1) * 128] if dch < 4 else zc[gp][:, dch - 4, tt * 128:(tt + 1) * 128]
                        fns.append(I(nc.tensor.matmul, gen[g][:, :], lhsT=src, rhs=wout[:, dch, nh * 512:(nh + 1) * 512],
                                     start=(dch == 0), stop=(dch == 7)))
                    P.emit("pe", fns, reads=b_za[gp] + b_zc[gp] + b_wout, writes=[b_gen[g]])
                    P.emit("dve", I(nc.vector.tensor_tensor, out=xres[xr][:], in0=gen[g][:, :], in1=xres[xr][:], op=ALU.add),
                           reads=[b_gen[g]], writes=[b_xres[xr]])
                    store_toks[xr] = P.dma("pool", I(nc.gpsimd.dma_start, out=out_d[bi, r0:r0 + 128, nh * 512:(nh + 1) * 512],
                                                     in_=xres[xr][:]), f"st{xr}", reads=[b_xres[xr]])

        epsb = sb("epsb", [128, 2], F32)
        b_eps = Buf()
        P.emit("pool", I(nc.gpsimd.memset, epsb[:, 0:1], EPS), writes=[b_eps])
        P.emit("pool", I(nc.gpsimd.memset, epsb[:, 1:2], -0.5), writes=[b_eps])

        for bi in range(2):
            P.mark(3 + 10 * bi)
            on = 0
            P.emit("pool", I(nc.gpsimd.memset, fs[on][:, 0:128], 1.0), writes=[b_fs[on]])
            for nh in range(2):
                for q4 in range(4):
                    ch = nh * 4 + q4
                    dg = 1 + (ch % 7)
                    P.emit("dve", I(nc.vector.tensor_scalar, out=fs[dg][:, 0:128], in0=ident, scalar1=modsb[:, 16 + ch, bi:bi + 1],
                                    scalar2=None, op0=ALU.mult), reads=[b_cbf, b_modsb], writes=[b_fs[dg]])
                    P.emit("pe", I(nc.tensor.matmul, Ob[nh][:, q4 * 128:(q4 + 1) * 128], lhsT=fs[on][:, 0:128], rhs=fs[dg][:, 0:128],
                                   start=True, stop=True), reads=[b_fs[on], b_fs[dg]], writes=[b_O[nh]])
            WST = [2, 3, 6, 7]

            def wout_dma(k):
                for kc in (2 * k, 2 * k + 1):
                    for nh in range(2):
                        st_ = WST[(kc % 2) * 2 + nh]
                        P.dma("sp", I(nc.sync.dma_start, out=fs[st_][:, :], in_=wout_d[kc * 128:(kc + 1) * 128, nh * 512:(nh + 1) * 512]),
                              f"wst{st_}", writes=[b_fs[st_]])

            def wout_mul(k):
                for kc in (2 * k, 2 * k + 1):
                    for nh in range(2):
                        st_ = WST[(kc % 2) * 2 + nh]
                        P.emit("dve", I(nc.vector.tensor_tensor, out=wout[:, kc, nh * 512:(nh + 1) * 512],
                                        in0=Ob[nh][:, :], in1=fs[st_][:, :], op=ALU.mult),
                               reads=[b_O[nh], b_fs[st_]], writes=[b_wout[kc]])

            P.mark(4 + 10 * bi)
            def lat_srcs(g):
                return [x_d[bi, g * 512 + tt * 128: g * 512 + (tt + 1) * 128, :] for tt in range(4)]
            kvg = [(2, 2, 0, 0, False, [ctx_d[bi, tt * 128:(tt + 1) * 128, :] for tt in range(2)])]
            for g in range(4):
                kvg.append((4, bi, 256 + g * 512, 1 + g, True, lat_srcs(g)))
            xphase(kvg[0][5], "mix")
            prev = None
            for i, (nt, j, koff, kidx, rope, srcs) in enumerate(kvg):
                L = i % 2
                nxt = kvg[i + 1][5] if i + 1 < len(kvg) else lat_srcs(0)
                kv_front(nt, j, koff, L, after_t=lambda nxt=nxt: xphase(nxt, "mix"))
                if i >= 1:
                    wout_mul(i - 1)
                if i <= 3:
                    wout_dma(i)
                if rope:
                    load_rope(i - 1)
                kv_back(nt, koff, kidx, rope, L)

            if dbg and bi == 0:
                def dcp(name, src_ap, n, bufs):
                    t = 0
                    while t < n:
                        k = rr_fs.next()
                        P.emit("dve", I(nc.vector.tensor_copy, out=fs[k][:, :], in_=src_ap[:, t:t + 512]), reads=bufs, writes=[b_fs[k]])
                        P.dma("sp", I(nc.sync.dma_start, out=dbg_d[name][:, t:t + 512], in_=fs[k][:, :]), "dbg", reads=[b_fs[k]])
                        t += 512
                dcp("d_KT", KT[:, 0:2048], 2048, b_KT)
                dcp("d_V", Vaug[:].rearrange("p a b c -> p (a b c)"), NKT * 256, b_V)

            P.mark(6 + 10 * bi)
            def side_units(g):
                gp_ = g % 2
                units = []
                for cc in range(4):
                    def u_g(cc=cc):
                        gg_ = proj(1792 + cc * 128, 512, [b_win[3]])
                        sg = 4 + (cc % 2)
                        P.emit("act", I(nc.scalar.activation, out=bs[sg][:, :], in_=gen[gg_][:, :], func=AF.Sigmoid),
                               reads=[b_gen[gg_]], writes=[b_bs[sg]])
                        ga = proj(1280 + cc * 128, 512, [b_win[2]])
                        P.emit("dve", I(nc.vector.tensor_tensor, out=uconv[:, cc, 15 + g * 512: 15 + (g + 1) * 512],
                                        in0=gen[ga][:, :], in1=bs[sg][:, :], op=ALU.mult),
                               reads=[b_gen[ga], b_bs[sg]], writes=[b_uc[cc][g]])
                    units.append(u_g)
                for cc in range(4):
                    def u_zc(cc=cc):
                        gz = proj(2304 + cc * 128, 512, [b_win[3]])
                        P.emit("act", I(nc.scalar.activation, out=zc[gp_][:, cc, :], in_=gen[gz][:, :], func=AF.Silu),
                               reads=[b_gen[gz]], writes=[b_zc[gp_][cc]])
                    units.append(u_zc)
                for c in range(4):
                    def u_za(c=c):
                        gz = proj(768 + c * 128, 512, [b_win[2]])
                        P.emit("act", I(nc.scalar.activation, out=za[gp_][:, c, :], in_=gen[gz][:, :], func=AF.Silu),
                               reads=[b_gen[gz]], writes=[b_za[gp_][c]])
                    units.append(u_za)
                return units

            tphase(4, bi)
            xphase(lat_srcs(1))
            for g in range(4):
                gp = g % 2
                load_rope(g)
                for c in range(4):
                    gq = proj(c * 128, 512, [b_win[1]])
                    qk_A(gq, 512, c)
                for c in range(4):
                    qk_B(512, c)
                if g >= 1:
                    prebuild_diags()
                su = side_units(g)
                for u in su[0:4]:
                    u()
                for c in range(4):
                    qk_C(512, c, vcol(V_QNW), True, [(slice(0, 64), QT[0:64, c, 0, :], b_QT[c]),
                                                     (slice(64, 128), QT[64:128, c, 1, :], b_QT[c])])
                if dbg and bi == 0 and g == 0:
                    dcp("d_QT", QT[:].rearrange("p a b c -> p (a b c)"), 4096, b_QT)
                P.mark(7 + 10 * bi)
                if g >= 1:
                    conv_p1(g - 1)
                for u in su[4:8]:
                    u()
                if g >= 1:
                    conv_p2a()
                for u in su[8:12]:
                    u()
                if g < 3:
                    tphase(4, bi)
                if g >= 1:
                    conv_p2b(1 - gp)
                    if dbg and bi == 0 and g == 1:
                        dcp("d_zc", zc[1 - gp][:].rearrange("p a b -> p (a b)"), 2048, b_zc[1 - gp])
                        dcp("d_za", za[1 - gp][:].rearrange("p a b -> p (a b)"), 2048, b_za[1 - gp])
                    outproj(bi, g - 1, 1 - gp)
                P.mark(8 + 10 * bi)
                hook = (lambda gn=g + 2: xphase(lat_srcs(gn), "dve", "pool")) if g + 2 <= 3 else None
                attention(gp, hook)
                P.mark(6 + 10 * bi)
            P.mark(9 + 10 * bi)
            conv_block(3, 1)
            outproj(bi, 3, 1)

        P.enabled = True
        P.wait_only("sp", [t for t in store_toks.values() if t])
        P.wait_only("pool", [t for t in store_toks.values() if t])
        P.build(nc, st)
    return nc


def _perm_heads():
    idx = []
    for c in range(4):
        for half in range(2):
            h = c + 4 * half
            idx.extend(range(h * 64, (h + 1) * 64))
    return np.array(idx)


def _rope_tables():
    S = S_LAT
    t = np.arange(S)
    row = (t // 64).astype(np.float64)
    col = (t % 64).astype(np.float64)
    freqs = 10000.0 ** (-np.arange(0, 32, 2, dtype=np.float64) / 32.0)
    ang_r = row[:, None] * freqs[None, :]
    ang_c = col[:, None] * freqs[None, :]
    cosT = np.zeros((128, S), np.float32)
    sinT = np.zeros((128, S), np.float32)
    for p in range(128):
        d = p % 64
        if d < 32:
            a = ang_r[:, d % 16]
        else:
            a = ang_c[:, (d - 32) % 16]
        cosT[p] = np.cos(a)
        sinT[p] = np.sin(a)
    return cosT, sinT


def _consts():
    c = np.zeros((128, 512), np.float32)
    c[:, 0:128] = np.eye(128, dtype=np.float32)
    R = np.zeros((128, 128), np.float32)
    for m in range(128):
        if (m % 32) < 16:
            R[m + 16, m] = -1.0
        else:
            R[m - 16, m] = 1.0
    c[:, 128:256] = R
    bo = np.zeros((128, 128), np.float32)
    bo[0:64, 0:64] = 1.0 / 64.0
    bo[64:128, 64:128] = 1.0 / 64.0
    c[:, 256:384] = bo
    c[:, 384:512] = 1.0 / 512.0
    sel = np.zeros((3, 256), np.float32)
    sel[0, 0:128] = 1.0
    sel[1, 128:256] = 1.0
    return c, sel


_NC_CACHE = {}


def _prep(inputs):
    f = lambda a: np.ascontiguousarray(np.asarray(a, dtype=np.float32))
    x = f(inputs["x"]); c = f(inputs["c"]); ctx = f(inputs["ctx"]); c_ctx = f(inputs["c_ctx"])
    w_mod = f(inputs["w_mod"])[0]; b_mod = f(inputs["b_mod"])[0]; norm_w = f(inputs["norm_w"])[0]
    w_in = f(inputs["w_in"])[0]; qn = f(inputs["q_norm_w"])[0]; kn = f(inputs["k_norm_w"])[0]
    conv_w = f(inputs["conv_w"])[0]; conv_b = f(inputs["conv_b"])[0]
    ln_w = f(inputs["conv_ln_w"])[0]; ln_b = f(inputs["conv_ln_b"])[0]
    w_pw = f(inputs["w_pw"])[0]; b_pw = f(inputs["b_pw"])[0]; w_out = f(inputs["w_out"])[0]
    perm = _perm_heads()
    cols = np.arange(2816)
    cols[0:512] = perm
    cols[768:1280] = 768 + perm
    w_in_p = np.ascontiguousarray(w_in[:, cols])
    rows = np.arange(1024)
    rows[0:512] = perm
    w_out_p = np.ascontiguousarray(w_out[rows, :])
    vecs = np.zeros((128, NV), np.float32)
    vecs[:, V_NORMW:V_NORMW + 8] = norm_w.reshape(8, 128).T
    vecs[:, V_BMOD:V_BMOD + 24] = b_mod.reshape(24, 128).T
    vecs[:, V_QNW] = np.tile(qn, 2)
    vecs[:, V_KNW] = np.tile(kn, 2)
    vecs[:, V_CB:V_CB + 4] = conv_b.reshape(4, 128).T
    vecs[:, V_LNW:V_LNW + 4] = ln_w.reshape(4, 128).T
    vecs[:, V_LNB:V_LNB + 4] = ln_b.reshape(4, 128).T
    vecs[:, V_BPW:V_BPW + 4] = b_pw.reshape(4, 128).T
    vecs[:, V_CW:V_CW + 124] = conv_w.reshape(31, 4, 128).transpose(2, 1, 0).reshape(128, 124)
    constsf = np.concatenate([np.eye(128, dtype=np.float32), np.ones((128, 128), np.float32)], axis=1)
    consts, sel = _consts()
    cosT, sinT = _rope_tables()
    in_maps = []
    for i in range(8):
        cc = np.stack([c[2 * i], c[2 * i + 1], c_ctx], axis=1)
        cT = np.ascontiguousarray(cc.reshape(8, 128, 3).transpose(1, 0, 2).reshape(128, 24))
        in_maps.append({
            "x": np.ascontiguousarray(x[2 * i:2 * i + 2]), "ctx": np.ascontiguousarray(ctx[2 * i:2 * i + 2]),
            "cT": cT, "w_mod": w_mod, "w_in": w_in_p, "w_pw": w_pw, "w_out": w_out_p, "vecs": vecs,
            "consts": consts, "cosT": cosT, "sinT": sinT,
        })
    return in_maps


def kernel(**inputs):
    in_maps = _prep(inputs)
    if "nc" not in _NC_CACHE:
        _NC_CACHE["nc"] = build_nc()
    nc = _NC_CACHE["nc"]
    res = run_bass_kernel_spmd(nc, in_maps, core_ids=list(range(8)))
    return np.concatenate([np.asarray(r["out"], dtype=np.float32) for r in res.results], axis=0)
```

```python
import contextlib
import numpy as np
import concourse.bass as bass
import concourse.mybir as mybir
from concourse.bass_utils import run_bass_kernel_spmd

F32 = mybir.dt.float32
BF16 = mybir.dt.bfloat16
AF = mybir.ActivationFunctionType
ALU = mybir.AluOpType

ENGS = ["pe", "act", "dve", "pool", "sp"]
import os
SAME_ENGINE_SYNC = not os.environ.get("NOSES")
EPS = 1e-6
S_LAT = 2048
C_LEN = 256
NKT = (S_LAT + C_LEN) // 128

V_NORMW = 0
V_BMOD = 8
V_QNW = 32
V_KNW = 33
V_CB = 34
V_LNW = 38
V_LNB = 42
V_BPW = 46
V_CW = 50
NV = 50 + 124


class Buf:
    __slots__ = ("name", "w", "r")

    def __init__(self, name=""):
        self.name = name
        self.w = None
        self.r = []


def I(fn, *a, **k):
    return lambda: fn(*a, **k)


class Prog:
    def __init__(self):
        self.ops = {e: [] for e in ENGS}
        self.cnt = {}
        self.waited = {e: {} for e in ENGS}
        self.semkeys = []
        self.glob = []
        self.enabled = True
        self.stage = 0
        self.stop_stage = 10 ** 9

    def sub(self, x):
        if self.enabled and self.base + x > self.stop_stage:
            self.enabled = False

    def mark(self, n):
        self.base = n
        self.stage = n
        if n > self.stop_stage:
            self.enabled = False

    def _sem(self, key):
        if key not in self.cnt:
            self.cnt[key] = 0
            self.semkeys.append(key)

    def _waits(self, eng, reads, writes, extra):
        deps = {}

        def add(tok):
            if tok is None:
                return
            k, v = tok
            if deps.get(k, 0) < v:
                deps[k] = v
        for b in reads:
            add(b.w)
        for b in writes:
            if not b.r:
                add(b.w)
            for t in b.r:
                add(t)
        for t in extra:
            add(t)
        out = []
        for k, v in deps.items():
            if k == eng and (not SAME_ENGINE_SYNC or eng == "pe"):
                continue
            if self.waited[eng].get(k, 0) >= v:
                continue
            self.waited[eng][k] = v
            out.append((k, v))
        return out

    def emit(self, eng, fns, reads=(), writes=(), extra=()):
        if not self.enabled:
            return None
        if callable(fns):
            fns = [fns]
        self._sem(eng)
        waits = self._waits(eng, reads, writes, extra)
        self.cnt[eng] += 1
        tok = (eng, self.cnt[eng])
        self.glob.append((eng, (waits, list(fns), (eng, 1))))
        for b in reads:
            b.r.append(tok)
        for b in writes:
            b.w = tok
            b.r = []
        return tok

    def dma(self, queue, fn, semkey, reads=(), writes=(), extra=()):
        if not self.enabled:
            return None
        self._sem(semkey)
        waits = self._waits(queue, reads, writes, extra)
        self.cnt[semkey] += 16
        tok = (semkey, self.cnt[semkey])
        self.glob.append((queue, (waits, [fn], (semkey, 16))))
        for b in reads:
            b.r.append(tok)
        for b in writes:
            b.w = tok
            b.r = []
        return tok

    def wait_only(self, eng, toks):
        waits = self._waits(eng, (), (), toks)
        if waits:
            self.glob.append((eng, (waits, [], None)))

    def build(self, nc, stack, max_ins=10 ** 9):
        sems = {}
        for k in self.semkeys:
            sems[k] = stack.enter_context(nc.semaphore("s_" + k))
        chunks = []
        cur = {e: [] for e in ENGS}
        cnt = {e: 0 for e in ENGS}
        for eng, op in self.glob:
            n = len(op[0]) + len(op[1])
            if cnt[eng] + n > max_ins and cnt[eng] > 0:
                chunks.append(cur)
                cur = {e: [] for e in ENGS}
                cnt = {e: 0 for e in ENGS}
            cur[eng].append(op)
            cnt[eng] += n
        chunks.append(cur)
        self.nchunks = len(chunks)

        def make(ops):
            def body(engine):
                for waits, fns, inc in ops:
                    for k, v in waits:
                        engine.wait_ge(sems[k], v)
                    for i, fn in enumerate(fns):
                        ins = fn()
                        if i == len(fns) - 1 and inc is not None:
                            ins.then_inc(sems[inc[0]], inc[1])
            return body

        for ch in chunks:
            with nc.Block() as block:
                engdec = {"pe": block.tensor, "act": block.scalar, "dve": block.vector,
                          "pool": block.gpsimd, "sp": block.sync}
                for e in ENGS:
                    if ch[e]:
                        engdec[e](make(ch[e]))


class RR:
    def __init__(self, n):
        self.n = n
        self.i = -1

    def next(self):
        self.i = (self.i + 1) % self.n
        return self.i


def build_nc(dbg=False, stop_stage=10 ** 9):
    nc = bass.Bass("TRN2", target_bir_lowering=False)
    x_d = nc.dram_tensor("x", [2, S_LAT, 1024], F32, kind="ExternalInput").ap()
    ctx_d = nc.dram_tensor("ctx", [2, C_LEN, 1024], F32, kind="ExternalInput").ap()
    cT_d = nc.dram_tensor("cT", [128, 24], F32, kind="ExternalInput").ap()
    wmod_d = nc.dram_tensor("w_mod", [1024, 3072], F32, kind="ExternalInput").ap()
    win_d = nc.dram_tensor("w_in", [1024, 2816], F32, kind="ExternalInput").ap()
    wpw_d = nc.dram_tensor("w_pw", [512, 512], F32, kind="ExternalInput").ap()
    wout_d = nc.dram_tensor("w_out", [1024, 1024], F32, kind="ExternalInput").ap()
    vecs_d = nc.dram_tensor("vecs", [128, NV], F32, kind="ExternalInput").ap()
    consts_d = nc.dram_tensor("consts", [128, 512], F32, kind="ExternalInput").ap()
    cos_d = nc.dram_tensor("cosT", [128, S_LAT], F32, kind="ExternalInput").ap()
    sin_d = nc.dram_tensor("sinT", [128, S_LAT], F32, kind="ExternalInput").ap()
    out_d = nc.dram_tensor("out", [2, S_LAT, 1024], F32, kind="ExternalOutput").ap()
    dbg_d = {}
    if dbg:
        for name, shp in [("d_uT", [128, 8 * 512]), ("d_KT", [128, 2304]), ("d_V", [128, NKT * 256]),
                          ("d_QT", [128, 8 * 512]), ("d_za", [128, 4 * 512]), ("d_zc", [128, 4 * 512]),
                          ("d_uc", [128, 4 * 2078]), ("d_AB", [128, 48])]:
            dbg_d[name] = nc.dram_tensor(name, shp, F32, kind="ExternalOutput").ap()

    P = Prog()
    P.stop_stage = stop_stage
    st = contextlib.ExitStack()
    with st:
        def sb(name, shape, dt):
            return st.enter_context(nc.sbuf_tensor("sb_" + name, shape, dt))

        def ps(name, shape, dt):
            return st.enter_context(nc.psum_tensor("ps_" + name, shape, dt))

        win = sb("win", [128, 8, 2816], BF16)
        b_win = [Buf() for _ in range(4)]
        wout = sb("wout", [128, 8, 1024], BF16)
        b_wout = [Buf() for _ in range(8)]
        wpw = sb("wpw", [128, 4, 512], BF16)
        b_wpw = Buf()
        diag = [sb(f"diag{i}", [128, 31, 128], BF16) for i in range(2)]
        b_diag = [Buf() for _ in range(2)]
        rr_diag = RR(2)
        uT = sb("uT", [128, 8, 512], BF16)
        b_uT = [Buf() for _ in range(8)]
        xn = sb("xn", [128, 4, 1024], BF16)
        b_xn = [Buf() for _ in range(4)]
        xt = [sb(f"xt{i}", [128, 1024], F32) for i in range(3)]
        b_xt = [Buf() for _ in range(3)]
        rr_xt = RR(3)
        QT = sb("QT", [128, 4, 2, 512], BF16)
        b_QT = [Buf() for _ in range(4)]
        KT = sb("KT", [128, 2304], BF16)
        b_KT = [Buf() for _ in range(5)]
        Vaug = sb("Vaug", [128, NKT, 2, 128], BF16)
        b_V = [Buf() for _ in range(NKT)]
        za = [sb(f"za{i}", [128, 4, 512], BF16) for i in range(2)]
        b_za = [[Buf() for _ in range(4)] for _ in range(2)]
        zc = [sb(f"zc{i}", [128, 4, 512], BF16) for i in range(2)]
        b_zc = [[Buf() for _ in range(4)] for _ in range(2)]
        uconv = sb("uconv", [128, 4, 2078], BF16)
        b_uc = [[Buf() for _ in range(4)] for _ in range(4)]
        b_ucpad = Buf()
        NPT = 3
        PT = [sb(f"PT{i}", [128, 1024], BF16) for i in range(NPT)]
        b_PT = [Buf() for _ in range(NPT)]
        rr_PT = RR(NPT)
        NF = 8
        fs = [sb(f"fs{i}", [128, 512], F32) for i in range(NF)]
        b_fs = [Buf() for _ in range(NF)]
        rr_fs = RR(NF)
        NB = 6
        bs = [sb(f"bs{i}", [128, 512], BF16) for i in range(NB)]
        b_bs = [Buf() for _ in range(NB)]
        rr_bs = RR(NB)
        ybf = sb("ybf", [128, 4, 512], BF16)
        b_ybf = [Buf() for _ in range(4)]
        y2 = sb("y2", [128, 4, 512], BF16)
        y2f = y2[:].rearrange("p a b -> p (a b)")
        b_y2 = [Buf() for _ in range(4)]
        xres = [sb(f"xres{i}", [128, 512], F32) for i in range(2)]
        b_xres = [Buf() for _ in range(2)]
        rr_xres = RR(2)
        cosb, sinb = xres[0], xres[1]
        b_cos, b_sin = b_xres[0], b_xres[1]
        cbf = sb("cbf", [128, 512], BF16)
        b_cbf = Buf()
        ident = cbf[:, 0:128]
        Rm = cbf[:, 128:256]
        bones = cbf[:, 256:384]
        odiv = cbf[:, 384:512]
        vecs = sb("vecs", [128, NV], F32)
        b_vecs = Buf()
        cT = sb("cT", [128, 24], F32)
        b_cT = Buf()
        scT = sb("scT", [128, 24], BF16)
        b_scT = Buf()
        modsb = sb("modsb", [128, 24, 3], F32)
        b_modsb = Buf()
        Asb = sb("Asb", [128, 8, 3], F32)
        b_A = Buf()
        NST = 8
        stat = sb("stat", [128, NST, 4], F32)
        b_stat = [Buf() for _ in range(NST)]
        rr_stat = RR(NST)

        NG = 2
        gen = [ps(f"gen{i}", [128, 512], F32) for i in range(NG)]
        b_gen = [Buf() for _ in range(NG)]
        rr_gen = RR(NG)
        Sb = [ps(f"S{i}", [128, 1024], F32) for i in range(2)]
        b_S = [Buf() for _ in range(2)]
        rr_S = RR(2)
        Tbs = [Sb[0][:, 0:512].bitcast(BF16), Sb[1][:, 0:512].bitcast(BF16)]
        b_T = b_S
        rr_T = RR(2)
        Ob = [ps(f"O{i}", [128, 512], F32) for i in range(2)]
        b_O = [Buf() for _ in range(2)]
        rr_O = RR(2)

        def vcol(off, n=1):
            return vecs[:, off:off + n]

        P.dma("sp", I(nc.sync.dma_start, out=vecs[:], in_=vecs_d[:, :]), "c_vecs", writes=[b_vecs])
        P.dma("sp", I(nc.sync.dma_start, out=cT[:], in_=cT_d[:, :]), "c_cT", writes=[b_cT])
        P.dma("pool", I(nc.gpsimd.dma_start, out=cbf[:], in_=consts_d[:, :]), "c_cbf", writes=[b_cbf])
        win_v = win_d.rearrange("(kc p) n -> p kc n", p=128)
        pieces = [(512, 768), (0, 512), (768, 1792), (1792, 2816)]
        def load_w(i):
            lo, hi = pieces[i]
            P.dma("pool", I(nc.gpsimd.dma_start, out=win[:, :, lo:hi], in_=win_v[:, :, lo:hi]), f"w_in{i}",
                  writes=[b_win[i]], extra=wmod_toks)
        wmod_toks = []
        load_w(0)
        P.emit("pool", I(nc.gpsimd.memset, uconv[:, :, 0:15], 0.0), writes=[b_ucpad])
        P.emit("pool", I(nc.gpsimd.memset, uconv[:, :, 2063:2078], 0.0), writes=[b_ucpad])
        P.emit("pool", I(nc.gpsimd.memset, Vaug[:], 1.0), writes=b_V)
        P.emit("pool", I(nc.gpsimd.memset, QT[:], 0.0), writes=b_QT)

        P.mark(1)
        P.emit("act", I(nc.scalar.activation, out=scT[:], in_=cT[:], func=AF.Silu), reads=[b_cT], writes=[b_scT])
        scT3 = scT[:].rearrange("p (k j) -> p k j", j=3)
        modps = gen[0]
        first = True
        wmod_last = {}
        for kc in range(8):
            for third in range(3):
                s = rr_xt.next()
                xtb = xt[s][:].bitcast(BF16)[:, 0:1024]
                tk = P.dma("pool", I(nc.gpsimd.dma_start, out=xtb, in_=wmod_d[kc * 128:(kc + 1) * 128, third * 1024:(third + 1) * 1024]),
                           f"wm{s}", writes=[b_xt[s]])
                wmod_last[s] = tk
                fns = []
                for jj in range(8):
                    col = (third * 8 + jj) * 3
                    fns.append(I(nc.tensor.matmul, modps[:, col:col + 3], lhsT=xtb[:, jj * 128:(jj + 1) * 128],
                                 rhs=scT3[:, kc, :], start=first, stop=(kc == 7 and third == 2 and jj == 7),
                                 skip_group_check=True))
                    first = False
                P.emit("pe", fns, reads=[b_xt[s], b_scT], writes=[b_gen[0]])
        wmod_toks = list(wmod_last.values())
        for i in (1, 2, 3):
            load_w(i)
        P.dma("pool", I(nc.gpsimd.dma_start, out=wpw[:], in_=wpw_d.rearrange("(cc p) n -> p cc n", p=128)),
              "w_pw", writes=[b_wpw])
        P.mark(2)
        P.emit("dve", I(nc.vector.tensor_tensor, out=modsb[:], in0=modps[:, 0:72].rearrange("p (c j) -> p c j", j=3),
                        in1=vcol(V_BMOD, 24).unsqueeze(2).to_broadcast([128, 24, 3]), op=ALU.add),
               reads=[b_gen[0], b_vecs], writes=[b_modsb])
        P.emit("dve", I(nc.vector.scalar_tensor_tensor, out=Asb[:], in0=modsb[:, 8:16, :], scalar=1.0,
                        in1=vcol(V_NORMW, 8).unsqueeze(2).to_broadcast([128, 8, 3]), op0=ALU.add, op1=ALU.mult),
               reads=[b_modsb, b_vecs], writes=[b_A])
        def Acol(kc, j):
            return Asb[:, kc, j:j + 1]

        def Bcol(kc, j):
            return modsb[:, kc, j:j + 1]

        def xphase(srcs, sq_eng="dve", rstd_eng="act"):
            for tt, src in enumerate(srcs):
                s = rr_xt.next()
                P.dma("sp", I(nc.sync.dma_start, out=xt[s][:], in_=src), f"xt{s}", writes=[b_xt[s]])
                q = rr_stat.next()
                if sq_eng == "act" or (sq_eng == "mix" and tt % 2 == 0):
                    P.emit("act", I(nc.scalar.activation, out=y2f[:, 0:1024], in_=xt[s][:], func=AF.Square,
                                    accum_out=stat[:, q, 0:1]), reads=[b_xt[s]], writes=b_y2[0:2] + [b_stat[q]])
                else:
                    P.emit("dve", I(nc.vector.scalar_tensor_tensor, out=y2f[:, 0:1024], in0=xt[s][:], scalar=1.0, in1=xt[s][:],
                                    op0=ALU.mult, op1=ALU.mult, accum_out=stat[:, q, 0:1]),
                           reads=[b_xt[s]], writes=b_y2[0:2] + [b_stat[q]])
                if rstd_eng == "pool":
                    P.emit("pool", I(nc.gpsimd.tensor_scalar, out=stat[:, q, 1:2], in0=stat[:, q, 0:1], scalar1=1.0 / 1024.0,
                                     scalar2=EPS, op0=ALU.mult, op1=ALU.add), reads=[b_stat[q]], writes=[b_stat[q]])
                    P.emit("pool", I(nc.gpsimd.tensor_tensor, out=stat[:, q, 2:3], in0=stat[:, q, 1:2], in1=epsb[:, 1:2], op=ALU.pow),
                           reads=[b_stat[q], b_eps], writes=[b_stat[q]])
                else:
                    P.emit("act", I(nc.scalar.activation, out=stat[:, q, 1:2], in_=stat[:, q, 0:1], func=AF.Ln,
                                    scale=1.0 / 1024.0, bias=epsb[:, 0:1]), reads=[b_stat[q], b_eps], writes=[b_stat[q]])
                    P.emit("act", I(nc.scalar.activation, out=stat[:, q, 2:3], in_=stat[:, q, 1:2], func=AF.Exp,
                                    scale=-0.5), reads=[b_stat[q]], writes=[b_stat[q]])
                P.emit("pool", I(nc.gpsimd.tensor_scalar, out=xn[:, tt, :], in0=xt[s][:], scalar1=stat[:, q, 2:3],
                                 scalar2=0.0, op0=ALU.mult, op1=ALU.add),
                       reads=[b_xt[s], b_stat[q]], writes=[b_xn[tt]])

        def tphase(nt, j):
            for kc in range(8):
                g = rr_T.next()
                tb = Tbs[g]
                fns = [I(nc.tensor.transpose, out=tb[:, tt * 128:(tt + 1) * 128], in_=xn[:, tt, kc * 128:(kc + 1) * 128],
                         identity=ident) for tt in range(nt)]
                P.emit("pe", fns, reads=b_xn[:nt] + [b_cbf], writes=[b_T[g]])
                P.emit("dve", I(nc.vector.tensor_scalar, out=uT[:, kc, 0:nt * 128], in0=tb[:, 0:nt * 128],
                                scalar1=Acol(kc, j), scalar2=Bcol(kc, j), op0=ALU.mult, op1=ALU.add),
                       reads=[b_T[g], b_A, b_modsb], writes=[b_uT[kc]])

        def proj(lo, N, wbufs):
            g = rr_gen.next()
            fns = [I(nc.tensor.matmul, gen[g][:, 0:N], lhsT=win[:, kc, lo:lo + 128], rhs=uT[:, kc, 0:N],
                     start=(kc == 0), stop=(kc == 7)) for kc in range(8)]
            P.emit("pe", fns, reads=b_uT + wbufs, writes=[b_gen[g]])
            return g

        def qk_A(g, N, L):
            P.emit("dve", I(nc.vector.tensor_copy, out=fs[L][:, 0:N], in_=gen[g][:, 0:N]),
                   reads=[b_gen[g]], writes=[b_fs[L]])
            P.emit("pool", I(nc.gpsimd.tensor_tensor, out=bs[L][:, 0:N], in0=fs[L][:, 0:N], in1=fs[L][:, 0:N], op=ALU.mult),
                   reads=[b_fs[L]], writes=[b_bs[L]])

        def qk_B(N, L):
            m = rr_gen.next()
            P.emit("pe", I(nc.tensor.matmul, gen[m][:, 0:N], lhsT=bones, rhs=bs[L][:, 0:N], start=True, stop=True),
                   reads=[b_bs[L], b_cbf], writes=[b_gen[m]])
            P.emit("act", I(nc.scalar.activation, out=fs[4 + L][:, 0:N], in_=gen[m][:, 0:N], func=AF.Ln, bias=epsb[:, 0:1]),
                   reads=[b_gen[m], b_eps], writes=[b_fs[4 + L]])
            P.emit("act", I(nc.scalar.activation, out=fs[4 + L][:, 0:N], in_=fs[4 + L][:, 0:N], func=AF.Exp, scale=-0.5),
                   reads=[b_fs[4 + L]], writes=[b_fs[4 + L]])

        def qk_C(N, L, wcol, rope, dests):
            raw, r = L, 4 + L
            if not rope:
                for rows, dest, dbuf in dests:
                    P.emit("dve", I(nc.vector.scalar_tensor_tensor, out=dest, in0=fs[raw][rows, 0:N], scalar=wcol[rows, :],
                                    in1=fs[r][rows, 0:N], op0=ALU.mult, op1=ALU.mult),
                           reads=[b_fs[raw], b_fs[r], b_vecs], writes=[dbuf])
                return
            P.emit("dve", I(nc.vector.scalar_tensor_tensor, out=bs[L][:, 0:N], in0=fs[raw][:, 0:N], scalar=wcol,
                            in1=fs[r][:, 0:N], op0=ALU.mult, op1=ALU.mult),
                   reads=[b_fs[raw], b_fs[r], b_vecs], writes=[b_bs[L]])
            m = rr_gen.next()
            P.emit("pe", I(nc.tensor.matmul, gen[m][:, 0:N], lhsT=Rm, rhs=bs[L][:, 0:N], start=True, stop=True),
                   reads=[b_bs[L], b_cbf], writes=[b_gen[m]])
            P.emit("pool", I(nc.gpsimd.tensor_tensor, out=fs[raw][:, 0:N], in0=bs[L][:, 0:N], in1=cosb[:, 0:N], op=ALU.mult),
                   reads=[b_bs[L], b_cos], writes=[b_fs[raw]])
            P.emit("dve", I(nc.vector.tensor_tensor, out=fs[r][:, 0:N], in0=gen[m][:, 0:N], in1=sinb[:, 0:N], op=ALU.mult),
                   reads=[b_gen[m], b_sin], writes=[b_fs[r]])
            for rows, dest, dbuf in dests:
                P.emit("pool", I(nc.gpsimd.tensor_tensor, out=dest, in0=fs[raw][rows, 0:N], in1=fs[r][rows, 0:N], op=ALU.add),
                       reads=[b_fs[raw], b_fs[r]], writes=[dbuf])

        def load_rope(g):
            P.dma("sp", I(nc.sync.dma_start, out=cosb[:], in_=cos_d[:, g * 512:(g + 1) * 512]), "cosb", writes=[b_cos])
            P.dma("sp", I(nc.sync.dma_start, out=sinb[:], in_=sin_d[:, g * 512:(g + 1) * 512]), "sinb", writes=[b_sin])

        def kv_front(nt, j, koff, L, after_t=None):
            N = nt * 128
            tphase(nt, j)
            if after_t is not None:
                after_t()
            g = proj(512, N, [b_win[0]])
            qk_A(g, N, L)
            qk_B(N, L)
            for tt in range(nt):
                gv = rr_gen.next()
                fns = [I(nc.tensor.matmul, gen[gv][:, 0:128], lhsT=uT[:, kc, tt * 128:(tt + 1) * 128],
                         rhs=win[:, kc, 640:768], start=(kc == 0), stop=(kc == 7)) for kc in range(8)]
                P.emit("pe", fns, reads=b_uT + [b_win[0]], writes=[b_gen[gv]])
                tile = koff // 128 + tt
                P.emit("act", I(nc.scalar.copy, out=Vaug[:, tile, 0, 0:64], in_=gen[gv][:, 0:64]),
                       reads=[b_gen[gv]], writes=[b_V[tile]])
                P.emit("act", I(nc.scalar.copy, out=Vaug[:, tile, 1, 64:128], in_=gen[gv][:, 64:128]),
                       reads=[b_gen[gv]], writes=[b_V[tile]])

        def kv_back(nt, koff, kidx, rope, L):
            N = nt * 128
            qk_C(N, L, vcol(V_KNW), rope, [(slice(0, 128), KT[:, koff:koff + N], b_KT[kidx])])

        def attention(gp, hook=None, bg=()):
            NP = NKT // 2
            items = [(c, half, pr) for c in range(4) for half in range(2) for pr in range(NP)]
            sbank = {}

            def emitS(idx):
                c, half, pr = items[idx]
                s = rr_S.next()
                sbank[idx] = s
                fns = []
                kset = set()
                for e in range(2):
                    kt = 2 * pr + e
                    kset.add(0 if kt < 2 else 1 + (kt - 2) // 4)
                    fns.append(I(nc.tensor.matmul, Sb[s][:, e * 512:(e + 1) * 512], lhsT=KT[:, kt * 128:(kt + 1) * 128],
                                 rhs=QT[:, c, half, :], start=True, stop=True))
                P.emit("pe", fns, reads=[b_KT[k] for k in kset] + [b_QT[c]], writes=[b_S[s]])
            SFIRST = not os.environ.get("NOSFIRST")
            emitS(0)
            if SFIRST:
                emitS(1)
            o = None
            bg = list(bg)
            bg_every = max(1, (len(items) - 8) // max(1, len(bg))) if bg else 0
            for idx, (c, half, pr) in enumerate(items):
                if idx == 8 and hook is not None:
                    hook()
                if bg and idx >= 4 and (idx - 4) % bg_every == 0:
                    bg.pop(0)()
                if not SFIRST and idx + 1 < len(items):
                    emitS(idx + 1)
                s = sbank.pop(idx)
                p = rr_PT.next()
                P.emit("act", I(nc.scalar.activation, out=PT[p][:, :], in_=Sb[s][:, :], func=AF.Exp, scale=0.125),
                       reads=[b_S[s]], writes=[b_PT[p]])
                if pr == 0:
                    o = rr_O.next()
                fns = []
                for e in range(2):
                    kt = 2 * pr + e
                    fns.append(I(nc.tensor.matmul, Ob[o][:, :], lhsT=Vaug[:, kt, half, :], rhs=PT[p][:, e * 512:(e + 1) * 512],
                                 start=(kt == 0), stop=(kt == NKT - 1)))
                if SFIRST and idx + 2 < len(items):
                    emitS(idx + 2)
                P.emit("pe", fns, reads=[b_V[2 * pr], b_V[2 * pr + 1], b_PT[p]], writes=[b_O[o]])
                if pr == NP - 1:
                    orow = slice(half * 64, half * 64 + 64)
                    srow = slice((1 - half) * 64, (1 - half) * 64 + 64)
                    ri = rr_fs.next()
                    P.emit("dve", I(nc.vector.reciprocal, out=fs[ri][srow, :], in_=Ob[o][srow, :]),
                           reads=[b_O[o]], writes=[b_fs[ri]])
                    ot = rr_fs.next()
                    P.emit("dve", I(nc.vector.tensor_tensor, out=fs[ot][orow, :], in0=Ob[o][orow, :], in1=fs[ri][srow, :],
                                    op=ALU.mult), reads=[b_O[o], b_fs[ri]], writes=[b_fs[ot]])
                    P.emit("pool", I(nc.gpsimd.tensor_tensor, out=za[gp][orow, c, :], in0=fs[ot][orow, :],
                                     in1=za[gp][orow, c, :], op=ALU.mult), reads=[b_fs[ot]], writes=[b_za[gp][c]])
            while bg:
                bg.pop(0)()

        cst = {}

        def build_diag(cc):
            d = rr_diag.next()
            fns = [I(nc.vector.tensor_scalar, out=diag[d][:, j, :], in0=ident, scalar1=vcol(V_CW + cc * 31 + j),
                     scalar2=None, op0=ALU.mult) for j in range(31)]
            P.emit("dve", fns, reads=[b_cbf, b_vecs], writes=[b_diag[d]])
            return d

        def prebuild_diags():
            cst["prebuilt"] = {0: build_diag(0), 1: build_diag(1)}

        def conv_block(gg, gp):
            conv_p1(gg)
            conv_p2a()
            conv_p2b(gp)

        def conv_p1(gg):
            mu_ps, msq_ps = Ob[0], Ob[1]
            for cc in range(4):
                if cc in cst.get("prebuilt", {}):
                    d = cst["prebuilt"].pop(cc)
                else:
                    d = build_diag(cc)
                g = rr_gen.next()
                fns = [I(nc.tensor.matmul, gen[g][:, :], lhsT=diag[d][:, j, :],
                         rhs=uconv[:, cc, gg * 512 + j: gg * 512 + j + 512], start=(j == 0), stop=(j == 30)) for j in range(31)]
                ucr = [b_uc[cc][k] for k in range(max(0, gg - 1), min(3, gg + 1) + 1)] + [b_ucpad]
                P.emit("pe", fns, reads=[b_diag[d]] + ucr, writes=[b_gen[g]])
                P.emit("act", I(nc.scalar.activation, out=ybf[:, cc, :], in_=gen[g][:, :], func=AF.Identity,
                                bias=vcol(V_CB + cc)), reads=[b_gen[g], b_vecs], writes=[b_ybf[cc]])
                sq = rr_bs.next()
                P.emit("pool", I(nc.gpsimd.tensor_tensor, out=bs[sq][:, :], in0=ybf[:, cc, :], in1=ybf[:, cc, :], op=ALU.mult),
                       reads=[b_ybf[cc]], writes=[b_bs[sq]])
                P.emit("pe", I(nc.tensor.matmul, mu_ps[:, :], lhsT=odiv, rhs=ybf[:, cc, :], start=(cc == 0), stop=(cc == 3)),
                       reads=[b_ybf[cc], b_cbf], writes=[b_O[0]])
                P.emit("pe", I(nc.tensor.matmul, msq_ps[:, :], lhsT=odiv, rhs=bs[sq][:, :], start=(cc == 0), stop=(cc == 3)),
                       reads=[b_bs[sq], b_cbf], writes=[b_O[1]])

        def conv_p2a():
            mu_ps, msq_ps = Ob[0], Ob[1]
            mu = rr_fs.next()
            P.emit("act", I(nc.scalar.copy, out=fs[mu][:, :], in_=mu_ps[:, :]), reads=[b_O[0]], writes=[b_fs[mu]])
            m2 = rr_fs.next()
            cst["mu"], cst["m2"] = mu, m2
            P.emit("pool", I(nc.gpsimd.tensor_tensor, out=fs[m2][:, :], in0=fs[mu][:, :], in1=fs[mu][:, :], op=ALU.mult),
                   reads=[b_fs[mu]], writes=[b_fs[m2]])
            P.emit("dve", I(nc.vector.tensor_tensor, out=fs[m2][:, :], in0=msq_ps[:, :], in1=fs[m2][:, :], op=ALU.subtract),
                   reads=[b_O[1], b_fs[m2]], writes=[b_fs[m2]])
            P.emit("act", I(nc.scalar.activation, out=fs[m2][:, :], in_=fs[m2][:, :], func=AF.Ln, bias=epsb[:, 0:1]),
                   reads=[b_fs[m2], b_eps], writes=[b_fs[m2]])
            P.emit("act", I(nc.scalar.activation, out=fs[m2][:, :], in_=fs[m2][:, :], func=AF.Exp, scale=-0.5),
                   reads=[b_fs[m2]], writes=[b_fs[m2]])

        def conv_p2b(gp):
            mu, m2 = cst["mu"], cst["m2"]
            g0, g1 = rr_gen.next(), rr_gen.next()
            banks = [(gen[g0], b_gen[g0]), (gen[g1], b_gen[g1]), (Ob[0], b_O[0]), (Ob[1], b_O[1])]
            for cc in range(4):
                t = rr_fs.next()
                P.emit("dve", I(nc.vector.tensor_tensor, out=fs[t][:, :], in0=ybf[:, cc, :], in1=fs[mu][:, :], op=ALU.subtract),
                       reads=[b_ybf[cc], b_fs[mu]], writes=[b_fs[t]])
                P.emit("dve", I(nc.vector.tensor_tensor, out=fs[t][:, :], in0=fs[t][:, :], in1=fs[m2][:, :], op=ALU.mult),
                       reads=[b_fs[t], b_fs[m2]], writes=[b_fs[t]])
                P.emit("act", I(nc.scalar.activation, out=y2[:, cc, :], in_=fs[t][:, :], func=AF.Silu,
                                scale=vcol(V_LNW + cc), bias=vcol(V_LNB + cc)), reads=[b_fs[t], b_vecs], writes=[b_y2[cc]])
                fns = [I(nc.tensor.matmul, banks[nn][0][:, :], lhsT=wpw[:, cc, nn * 128:(nn + 1) * 128], rhs=y2[:, cc, :],
                         start=(cc == 0), stop=(cc == 3)) for nn in range(4)]
                P.emit("pe", fns, reads=[b_y2[cc], b_wpw], writes=[bk[1] for bk in banks])
            for nn in range(4):
                P.emit("dve", I(nc.vector.scalar_tensor_tensor, out=zc[gp][:, nn, :], in0=banks[nn][0][:, :], scalar=vcol(V_BPW + nn),
                                in1=zc[gp][:, nn, :], op0=ALU.add, op1=ALU.mult),
                       reads=[banks[nn][1], b_vecs], writes=[b_zc[gp][nn]])

        store_toks = {}

        def outproj(bi, gg, gp):
            for tt in range(4):
                r0 = gg * 512 + tt * 128
                for nh in range(2):
                    xr = rr_xres.next()
                    P.dma("sp", I(nc.sync.dma_start, out=xres[xr][:], in_=x_d[bi, r0:r0 + 128, nh * 512:(nh + 1) * 512]),
                          f"xr{xr}", writes=[b_xres[xr]])
                    g = rr_gen.next()
                    fns = []
                    for dch in range(8):
                        src = za[gp][:, dch, tt * 128:(tt + 1) * 128] if dch < 4 else zc[gp][:, dch - 4, tt * 128:(tt + 1) * 128]
                        fns.append(I(nc.tensor.matmul, gen[g][:, :], lhsT=src, rhs=wout[:, dch, nh * 512:(nh + 1) * 512],
                                     start=(dch == 0), stop=(dch == 7)))
                    P.emit("pe", fns, reads=b_za[gp] + b_zc[gp] + b_wout, writes=[b_gen[g]])
                    P.emit("dve", I(nc.vector.tensor_tensor, out=xres[xr][:], in0=gen[g][:, :], in1=xres[xr][:], op=ALU.add),
                           reads=[b_gen[g]], writes=[b_xres[xr]])
                    store_toks[xr] = P.dma("pool", I(nc.gpsimd.dma_start, out=out_d[bi, r0:r0 + 128, nh * 512:(nh + 1) * 512],
                                                     in_=xres[xr][:]), f"st{xr}", reads=[b_xres[xr]])

        epsb = sb("epsb", [128, 2], F32)
        b_eps = Buf()
        P.emit("pool", I(nc.gpsimd.memset, epsb[:, 0:1], EPS), writes=[b_eps])
        P.emit("pool", I(nc.gpsimd.memset, epsb[:, 1:2], -0.5), writes=[b_eps])

        for bi in range(2):
            P.mark(3 + 10 * bi)
            on = 0
            P.emit("pool", I(nc.gpsimd.memset, fs[on][:, 0:128], 1.0), writes=[b_fs[on]])
            for nh in range(2):
                for q4 in range(4):
                    ch = nh * 4 + q4
                    dg = 1 + (ch % 7)
                    P.emit("dve", I(nc.vector.tensor_scalar, out=fs[dg][:, 0:128], in0=ident, scalar1=modsb[:, 16 + ch, bi:bi + 1],
                                    scalar2=None, op0=ALU.mult), reads=[b_cbf, b_modsb], writes=[b_fs[dg]])
                    P.emit("pe", I(nc.tensor.matmul, Ob[nh][:, q4 * 128:(q4 + 1) * 128], lhsT=fs[on][:, 0:128], rhs=fs[dg][:, 0:128],
                                   start=True, stop=True), reads=[b_fs[on], b_fs[dg]], writes=[b_O[nh]])
            WST = [2, 3, 6, 7]

            def wout_dma(k):
                for kc in (2 * k, 2 * k + 1):
                    for nh in range(2):
                        st_ = WST[(kc % 2) * 2 + nh]
                        P.dma("sp", I(nc.sync.dma_start, out=fs[st_][:, :], in_=wout_d[kc * 128:(kc + 1) * 128, nh * 512:(nh + 1) * 512]),
                              f"wst{st_}", writes=[b_fs[st_]])

            def wout_mul(k):
                for kc in (2 * k, 2 * k + 1):
                    for nh in range(2):
                        st_ = WST[(kc % 2) * 2 + nh]
                        P.emit("dve", I(nc.vector.tensor_tensor, out=wout[:, kc, nh * 512:(nh + 1) * 512],
                                        in0=Ob[nh][:, :], in1=fs[st_][:, :], op=ALU.mult),
                               reads=[b_O[nh], b_fs[st_]], writes=[b_wout[kc]])

            P.mark(4 + 10 * bi)
            def lat_srcs(g):
                return [x_d[bi, g * 512 + tt * 128: g * 512 + (tt + 1) * 128, :] for tt in range(4)]
            kvg = [(2, 2, 0, 0, False, [ctx_d[bi, tt * 128:(tt + 1) * 128, :] for tt in range(2)])]
            for g in range(4):
                kvg.append((4, bi, 256 + g * 512, 1 + g, True, lat_srcs(g)))
            xphase(kvg[0][5], "mix")
            prev = None
            for i, (nt, j, koff, kidx, rope, srcs) in enumerate(kvg):
                L = i % 2
                nxt = kvg[i + 1][5] if i + 1 < len(kvg) else lat_srcs(0)
                kv_front(nt, j, koff, L, after_t=lambda nxt=nxt: xphase(nxt, "mix"))
                if i >= 1:
                    wout_mul(i - 1)
                if i <= 3:
                    wout_dma(i)
                if rope:
                    load_rope(i - 1)
                kv_back(nt, koff, kidx, rope, L)

            if dbg and bi == 0:
                def dcp(name, src_ap, n, bufs):
                    t = 0
                    while t < n:
                        k = rr_fs.next()
                        P.emit("dve", I(nc.vector.tensor_copy, out=fs[k][:, :], in_=src_ap[:, t:t + 512]), reads=bufs, writes=[b_fs[k]])
                        P.dma("sp", I(nc.sync.dma_start, out=dbg_d[name][:, t:t + 512], in_=fs[k][:, :]), "dbg", reads=[b_fs[k]])
                        t += 512
                dcp("d_KT", KT[:, 0:2048], 2048, b_KT)
                dcp("d_V", Vaug[:].rearrange("p a b c -> p (a b c)"), NKT * 256, b_V)

            P.mark(6 + 10 * bi)
            def side_units(g):
                gp_ = g % 2
                units = []
                for cc in range(4):
                    def u_g(cc=cc):
                        gg_ = proj(1792 + cc * 128, 512, [b_win[3]])
                        sg = 4 + (cc % 2)
                        P.emit("act", I(nc.scalar.activation, out=bs[sg][:, :], in_=gen[gg_][:, :], func=AF.Sigmoid),
                               reads=[b_gen[gg_]], writes=[b_bs[sg]])
                        ga = proj(1280 + cc * 128, 512, [b_win[2]])
                        P.emit("dve", I(nc.vector.tensor_tensor, out=uconv[:, cc, 15 + g * 512: 15 + (g + 1) * 512],
                                        in0=gen[ga][:, :], in1=bs[sg][:, :], op=ALU.mult),
                               reads=[b_gen[ga], b_bs[sg]], writes=[b_uc[cc][g]])
                    units.append(u_g)
                for cc in range(4):
                    def u_zc(cc=cc):
                        gz = proj(2304 + cc * 128, 512, [b_win[3]])
                        P.emit("act", I(nc.scalar.activation, out=zc[gp_][:, cc, :], in_=gen[gz][:, :], func=AF.Silu),
                               reads=[b_gen[gz]], writes=[b_zc[gp_][cc]])
                    units.append(u_zc)
                for c in range(4):
                    def u_za(c=c):
                        gz = proj(768 + c * 128, 512, [b_win[2]])
                        P.emit("act", I(nc.scalar.activation, out=za[gp_][:, c, :], in_=gen[gz][:, :], func=AF.Silu),
                               reads=[b_gen[gz]], writes=[b_za[gp_][c]])
                    units.append(u_za)
                return units

            tphase(4, bi)
            xphase(lat_srcs(1))
            for g in range(4):
                gp = g % 2
                load_rope(g)
                for c in range(4):
                    gq = proj(c * 128, 512, [b_win[1]])
                    qk_A(gq, 512, c)
                for c in range(4):
                    qk_B(512, c)
                if g >= 1:
                    prebuild_diags()
                su = side_units(g)
                for u in su[0:4]:
                    u()
                for c in range(4):
                    qk_C(512, c, vcol(V_QNW), True, [(slice(0, 64), QT[0:64, c, 0, :], b_QT[c]),
                                                     (slice(64, 128), QT[64:128, c, 1, :], b_QT[c])])
                if dbg and bi == 0 and g == 0:
                    dcp("d_QT", QT[:].rearrange("p a b c -> p (a b c)"), 4096, b_QT)
                P.mark(7 + 10 * bi)
                if g >= 1:
                    conv_p1(g - 1)
                for u in su[4:8]:
                    u()
                if g >= 1:
                    conv_p2a()
                for u in su[8:12]:
                    u()
                if g < 3:
                    tphase(4, bi)
                if g >= 1:
                    conv_p2b(1 - gp)
                    if dbg and bi == 0 and g == 1:
                        dcp("d_zc", zc[1 - gp][:].rearrange("p a b -> p (a b)"), 2048, b_zc[1 - gp])
                        dcp("d_za", za[1 - gp][:].rearrange("p a b -> p (a b)"), 2048, b_za[1 - gp])
                    outproj(bi, g - 1, 1 - gp)
                P.mark(8 + 10 * bi)
                hook = (lambda gn=g + 2: xphase(lat_srcs(gn), "dve", "pool")) if g + 2 <= 3 else None
                attention(gp, hook)
                P.mark(6 + 10 * bi)
            P.mark(9 + 10 * bi)
            conv_block(3, 1)
            outproj(bi, 3, 1)

        P.enabled = True
        P.wait_only("sp", [t for t in store_toks.values() if t])
        P.wait_only("pool", [t for t in store_toks.values() if t])
        P.build(nc, st)
    return nc


def _perm_heads():
    idx = []
    for c in range(4):
        for half in range(2):
            h = c + 4 * half
            idx.extend(range(h * 64, (h + 1) * 64))
    return np.array(idx)


def _rope_tables():
    S = S_LAT
    t = np.arange(S)
    row = (t // 64).astype(np.float64)
    col = (t % 64).astype(np.float64)
    freqs = 10000.0 ** (-np.arange(0, 32, 2, dtype=np.float64) / 32.0)
    ang_r = row[:, None] * freqs[None, :]
    ang_c = col[:, None] * freqs[None, :]
    cosT = np.zeros((128, S), np.float32)
    sinT = np.zeros((128, S), np.float32)
    for p in range(128):
        d = p % 64
        if d < 32:
            a = ang_r[:, d % 16]
        else:
            a = ang_c[:, (d - 32) % 16]
        cosT[p] = np.cos(a)
        sinT[p] = np.sin(a)
    return cosT, sinT


def _consts():
    c = np.zeros((128, 512), np.float32)
    c[:, 0:128] = np.eye(128, dtype=np.float32)
    R = np.zeros((128, 128), np.float32)
    for m in range(128):
        if (m % 32) < 16:
            R[m + 16, m] = -1.0
        else:
            R[m - 16, m] = 1.0
    c[:, 128:256] = R
    bo = np.zeros((128, 128), np.float32)
    bo[0:64, 0:64] = 1.0 / 64.0
    bo[64:128, 64:128] = 1.0 / 64.0
    c[:, 256:384] = bo
    c[:, 384:512] = 1.0 / 512.0
    sel = np.zeros((3, 256), np.float32)
    sel[0, 0:128] = 1.0
    sel[1, 128:256] = 1.0
    return c, sel


_NC_CACHE = {}


def _prep(inputs):
    f = lambda a: np.ascontiguousarray(np.asarray(a, dtype=np.float32))
    x = f(inputs["x"]); c = f(inputs["c"]); ctx = f(inputs["ctx"]); c_ctx = f(inputs["c_ctx"])
    w_mod = f(inputs["w_mod"])[0]; b_mod = f(inputs["b_mod"])[0]; norm_w = f(inputs["norm_w"])[0]
    w_in = f(inputs["w_in"])[0]; qn = f(inputs["q_norm_w"])[0]; kn = f(inputs["k_norm_w"])[0]
    conv_w = f(inputs["conv_w"])[0]; conv_b = f(inputs["conv_b"])[0]
    ln_w = f(inputs["conv_ln_w"])[0]; ln_b = f(inputs["conv_ln_b"])[0]
    w_pw = f(inputs["w_pw"])[0]; b_pw = f(inputs["b_pw"])[0]; w_out = f(inputs["w_out"])[0]
    perm = _perm_heads()
    cols = np.arange(2816)
    cols[0:512] = perm
    cols[768:1280] = 768 + perm
    w_in_p = np.ascontiguousarray(w_in[:, cols])
    rows = np.arange(1024)
    rows[0:512] = perm
    w_out_p = np.ascontiguousarray(w_out[rows, :])
    vecs = np.zeros((128, NV), np.float32)
    vecs[:, V_NORMW:V_NORMW + 8] = norm_w.reshape(8, 128).T
    vecs[:, V_BMOD:V_BMOD + 24] = b_mod.reshape(24, 128).T
    vecs[:, V_QNW] = np.tile(qn, 2)
    vecs[:, V_KNW] = np.tile(kn, 2)
    vecs[:, V_CB:V_CB + 4] = conv_b.reshape(4, 128).T
    vecs[:, V_LNW:V_LNW + 4] = ln_w.reshape(4, 128).T
    vecs[:, V_LNB:V_LNB + 4] = ln_b.reshape(4, 128).T
    vecs[:, V_BPW:V_BPW + 4] = b_pw.reshape(4, 128).T
    vecs[:, V_CW:V_CW + 124] = conv_w.reshape(31, 4, 128).transpose(2, 1, 0).reshape(128, 124)
    constsf = np.concatenate([np.eye(128, dtype=np.float32), np.ones((128, 128), np.float32)], axis=1)
    consts, sel = _consts()
    cosT, sinT = _rope_tables()
    in_maps = []
    for i in range(8):
        cc = np.stack([c[2 * i], c[2 * i + 1], c_ctx], axis=1)
        cT = np.ascontiguousarray(cc.reshape(8, 128, 3).transpose(1, 0, 2).reshape(128, 24))
        in_maps.append({
            "x": np.ascontiguousarray(x[2 * i:2 * i + 2]), "ctx": np.ascontiguousarray(ctx[2 * i:2 * i + 2]),
            "cT": cT, "w_mod": w_mod, "w_in": w_in_p, "w_pw": w_pw, "w_out": w_out_p, "vecs": vecs,
            "consts": consts, "cosT": cosT, "sinT": sinT,
        })
    return in_maps


def kernel(**inputs):
    in_maps = _prep(inputs)
    if "nc" not in _NC_CACHE:
        _NC_CACHE["nc"] = build_nc()
    nc = _NC_CACHE["nc"]
    res = run_bass_kernel_spmd(nc, in_maps, core_ids=list(range(8)))
    return np.concatenate([np.asarray(r["out"], dtype=np.float32) for r in res.results], axis=0)
```

```python
import contextlib
import numpy as np
import concourse.bass as bass
import concourse.mybir as mybir
from concourse.bass_utils import run_bass_kernel_spmd

F32 = mybir.dt.float32
BF16 = mybir.dt.bfloat16
AF = mybir.ActivationFunctionType
ALU = mybir.AluOpType

ENGS = ["pe", "act", "dve", "pool", "sp"]
import os
SAME_ENGINE_SYNC = not os.environ.get("NOSES")
EPS = 1e-6
S_LAT = 2048
C_LEN = 256
NKT = (S_LAT + C_LEN) // 128

V_NORMW = 0
V_BMOD = 8
V_QNW = 32
V_KNW = 33
V_CB = 34
V_LNW = 38
V_LNB = 42
V_BPW = 46
V_CW = 50
NV = 50 + 124


class Buf:
    __slots__ = ("name", "w", "r")

    def __init__(self, name=""):
        self.name = name
        self.w = None
        self.r = []


def I(fn, *a, **k):
    return lambda: fn(*a, **k)


class Prog:
    def __init__(self):
        self.ops = {e: [] for e in ENGS}
        self.cnt = {}
        self.waited = {e: {} for e in ENGS}
        self.semkeys = []
        self.clock = {}
        self.glob = []
        self.enabled = True
        self.stage = 0
        self.stop_stage = 10 ** 9

    def sub(self, x):
        if self.enabled and self.base + x > self.stop_stage:
            self.enabled = False

    def mark(self, n):
        self.base = n
        self.stage = n
        if n > self.stop_stage:
            self.enabled = False

    def _sem(self, key):
        if key not in self.cnt:
            self.cnt[key] = 0
            self.semkeys.append(key)

    def _waits(self, eng, reads, writes, extra):
        deps = {}

        def add(tok):
            if tok is None:
                return
            k, v = tok
            if deps.get(k, 0) < v:
                deps[k] = v
        for b in reads:
            add(b.w)
        for b in writes:
            if not b.r:
                add(b.w)
            for t in b.r:
                add(t)
        for t in extra:
            add(t)
        out = []
        w = self.waited[eng]
        for k, v in sorted(deps.items(), key=lambda kv: -kv[1]):
            if k == eng and (not SAME_ENGINE_SYNC or eng == "pe"):
                continue
            if w.get(k, 0) >= v:
                continue
            w[k] = v
            out.append((k, v))
            for k2, v2 in self.clock.get((k, v), {}).items():
                if w.get(k2, 0) < v2:
                    w[k2] = v2
        return out

    def emit(self, eng, fns, reads=(), writes=(), extra=()):
        if not self.enabled:
            return None
        if callable(fns):
            fns = [fns]
        self._sem(eng)
        waits = self._waits(eng, reads, writes, extra)
        self.cnt[eng] += 1
        tok = (eng, self.cnt[eng])
        self.clock[tok] = dict(self.waited[eng])
        self.glob.append((eng, (waits, list(fns), (eng, 1))))
        for b in reads:
            b.r.append(tok)
        for b in writes:
            b.w = tok
            b.r = []
        return tok

    def dma(self, queue, fn, semkey, reads=(), writes=(), extra=()):
        if not self.enabled:
            return None
        self._sem(semkey)
        waits = self._waits(queue, reads, writes, extra)
        self.cnt[semkey] += 16
        tok = (semkey, self.cnt[semkey])
        self.clock[tok] = dict(self.waited[queue])
        self.glob.append((queue, (waits, [fn], (semkey, 16))))
        for b in reads:
            b.r.append(tok)
        for b in writes:
            b.w = tok
            b.r = []
        return tok

    def wait_only(self, eng, toks):
        waits = self._waits(eng, (), (), toks)
        if waits:
            self.glob.append((eng, (waits, [], None)))

    def build(self, nc, stack, max_ins=10 ** 9):
        sems = {}
        for k in self.semkeys:
            sems[k] = stack.enter_context(nc.semaphore("s_" + k))
        chunks = []
        cur = {e: [] for e in ENGS}
        cnt = {e: 0 for e in ENGS}
        for eng, op in self.glob:
            n = len(op[0]) + len(op[1])
            if cnt[eng] + n > max_ins and cnt[eng] > 0:
                chunks.append(cur)
                cur = {e: [] for e in ENGS}
                cnt = {e: 0 for e in ENGS}
            cur[eng].append(op)
            cnt[eng] += n
        chunks.append(cur)
        self.nchunks = len(chunks)

        def make(ops):
            def body(engine):
                for waits, fns, inc in ops:
                    for k, v in waits:
                        engine.wait_ge(sems[k], v)
                    for i, fn in enumerate(fns):
                        ins = fn()
                        if i == len(fns) - 1 and inc is not None:
                            ins.then_inc(sems[inc[0]], inc[1])
            return body

        for ch in chunks:
            with nc.Block() as block:
                engdec = {"pe": block.tensor, "act": block.scalar, "dve": block.vector,
                          "pool": block.gpsimd, "sp": block.sync}
                for e in ENGS:
                    if ch[e]:
                        engdec[e](make(ch[e]))


class RR:
    def __init__(self, n):
        self.n = n
        self.i = -1

    def next(self):
        self.i = (self.i + 1) % self.n
        return self.i


def build_nc(dbg=False, stop_stage=10 ** 9):
    nc = bass.Bass("TRN2", target_bir_lowering=False)
    x_d = nc.dram_tensor("x", [2, S_LAT, 1024], F32, kind="ExternalInput").ap()
    ctx_d = nc.dram_tensor("ctx", [2, C_LEN, 1024], F32, kind="ExternalInput").ap()
    cT_d = nc.dram_tensor("cT", [128, 24], F32, kind="ExternalInput").ap()
    wmod_d = nc.dram_tensor("w_mod", [1024, 3072], F32, kind="ExternalInput").ap()
    win_d = nc.dram_tensor("w_in", [1024, 2816], F32, kind="ExternalInput").ap()
    wpw_d = nc.dram_tensor("w_pw", [512, 512], F32, kind="ExternalInput").ap()
    wout_d = nc.dram_tensor("w_out", [1024, 1024], F32, kind="ExternalInput").ap()
    vecs_d = nc.dram_tensor("vecs", [128, NV], F32, kind="ExternalInput").ap()
    consts_d = nc.dram_tensor("consts", [128, 512], F32, kind="ExternalInput").ap()
    cos_d = nc.dram_tensor("cosT", [128, S_LAT], F32, kind="ExternalInput").ap()
    sin_d = nc.dram_tensor("sinT", [128, S_LAT], F32, kind="ExternalInput").ap()
    out_d = nc.dram_tensor("out", [2, S_LAT, 1024], F32, kind="ExternalOutput").ap()
    dbg_d = {}
    if dbg:
        for name, shp in [("d_uT", [128, 8 * 512]), ("d_KT", [128, 2304]), ("d_V", [128, NKT * 256]),
                          ("d_QT", [128, 8 * 512]), ("d_za", [128, 4 * 512]), ("d_zc", [128, 4 * 512]),
                          ("d_uc", [128, 4 * 2078]), ("d_AB", [128, 48])]:
            dbg_d[name] = nc.dram_tensor(name, shp, F32, kind="ExternalOutput").ap()

    P = Prog()
    P.stop_stage = stop_stage
    st = contextlib.ExitStack()
    with st:
        def sb(name, shape, dt):
            return st.enter_context(nc.sbuf_tensor("sb_" + name, shape, dt))

        def ps(name, shape, dt):
            return st.enter_context(nc.psum_tensor("ps_" + name, shape, dt))

        win = sb("win", [128, 8, 2816], BF16)
        b_win = [Buf() for _ in range(4)]
        wout = sb("wout", [128, 8, 1024], BF16)
        b_wout = [Buf() for _ in range(8)]
        wpw = sb("wpw", [128, 4, 512], BF16)
        b_wpw = Buf()
        diag = [sb(f"diag{i}", [128, 31, 128], BF16) for i in range(2)]
        b_diag = [Buf() for _ in range(2)]
        rr_diag = RR(2)
        uT = sb("uT", [128, 8, 512], BF16)
        b_uT = [Buf() for _ in range(8)]
        xn = sb("xn", [128, 4, 1024], BF16)
        b_xn = [Buf() for _ in range(4)]
        xt = [sb(f"xt{i}", [128, 1024], F32) for i in range(3)]
        b_xt = [Buf() for _ in range(3)]
        rr_xt = RR(3)
        QT = sb("QT", [128, 4, 2, 512], BF16)
        b_QT = [Buf() for _ in range(4)]
        KT = sb("KT", [128, 2304], BF16)
        b_KT = [Buf() for _ in range(5)]
        Vaug = sb("Vaug", [128, NKT, 2, 128], BF16)
        b_V = [Buf() for _ in range(NKT)]
        za = [sb(f"za{i}", [128, 4, 512], BF16) for i in range(2)]
        b_za = [[Buf() for _ in range(4)] for _ in range(2)]
        zc = [sb(f"zc{i}", [128, 4, 512], BF16) for i in range(2)]
        b_zc = [[Buf() for _ in range(4)] for _ in range(2)]
        uconv = sb("uconv", [128, 4, 2078], BF16)
        b_uc = [[Buf() for _ in range(4)] for _ in range(4)]
        b_ucpad = Buf()
        NPT = 3
        PT = [sb(f"PT{i}", [128, 1024], BF16) for i in range(NPT)]
        b_PT = [Buf() for _ in range(NPT)]
        rr_PT = RR(NPT)
        NF = 8
        fs = [sb(f"fs{i}", [128, 512], F32) for i in range(NF)]
        b_fs = [Buf() for _ in range(NF)]
        rr_fs = RR(NF)
        NB = 6
        bs = [sb(f"bs{i}", [128, 512], BF16) for i in range(NB)]
        b_bs = [Buf() for _ in range(NB)]
        rr_bs = RR(NB)
        ybf = sb("ybf", [128, 4, 512], BF16)
        b_ybf = [Buf() for _ in range(4)]
        y2 = sb("y2", [128, 4, 512], BF16)
        y2f = y2[:].rearrange("p a b -> p (a b)")
        b_y2 = [Buf() for _ in range(4)]
        xres = [sb(f"xres{i}", [128, 512], F32) for i in range(2)]
        b_xres = [Buf() for _ in range(2)]
        rr_xres = RR(2)
        cosb, sinb = xres[0], xres[1]
        b_cos, b_sin = b_xres[0], b_xres[1]
        cbf = sb("cbf", [128, 512], BF16)
        b_cbf = Buf()
        ident = cbf[:, 0:128]
        Rm = cbf[:, 128:256]
        bones = cbf[:, 256:384]
        odiv = cbf[:, 384:512]
        vecs = sb("vecs", [128, NV], F32)
        b_vecs = Buf()
        cT = sb("cT", [128, 24], F32)
        b_cT = Buf()
        scT = sb("scT", [128, 24], BF16)
        b_scT = Buf()
        modsb = sb("modsb", [128, 24, 3], F32)
        b_modsb = Buf()
        Asb = sb("Asb", [128, 8, 3], F32)
        b_A = Buf()
        NST = 8
        stat = sb("stat", [128, NST, 4], F32)
        b_stat = [Buf() for _ in range(NST)]
        rr_stat = RR(NST)

        NG = 2
        gen = [ps(f"gen{i}", [128, 512], F32) for i in range(NG)]
        b_gen = [Buf() for _ in range(NG)]
        rr_gen = RR(NG)
        Sb = [ps(f"S{i}", [128, 1024], F32) for i in range(2)]
        b_S = [Buf() for _ in range(2)]
        rr_S = RR(2)
        Tbs = [Sb[0][:, 0:512].bitcast(BF16), Sb[1][:, 0:512].bitcast(BF16)]
        b_T = b_S
        rr_T = RR(2)
        Ob = [ps(f"O{i}", [128, 512], F32) for i in range(2)]
        b_O = [Buf() for _ in range(2)]
        rr_O = RR(2)

        def vcol(off, n=1):
            return vecs[:, off:off + n]

        P.dma("sp", I(nc.sync.dma_start, out=vecs[:], in_=vecs_d[:, :]), "c_vecs", writes=[b_vecs])
        P.dma("sp", I(nc.sync.dma_start, out=cT[:], in_=cT_d[:, :]), "c_cT", writes=[b_cT])
        P.dma("pool", I(nc.gpsimd.dma_start, out=cbf[:], in_=consts_d[:, :]), "c_cbf", writes=[b_cbf])
        win_v = win_d.rearrange("(kc p) n -> p kc n", p=128)
        pieces = [(512, 768), (0, 512), (768, 1792), (1792, 2816)]
        def load_w(i):
            lo, hi = pieces[i]
            P.dma("pool", I(nc.gpsimd.dma_start, out=win[:, :, lo:hi], in_=win_v[:, :, lo:hi]), f"w_in{i}",
                  writes=[b_win[i]], extra=wmod_toks)
        wmod_toks = []
        load_w(0)
        P.emit("pool", I(nc.gpsimd.memset, uconv[:, :, 0:15], 0.0), writes=[b_ucpad])
        P.emit("pool", I(nc.gpsimd.memset, uconv[:, :, 2063:2078], 0.0), writes=[b_ucpad])
        P.emit("pool", I(nc.gpsimd.memset, Vaug[:], 1.0), writes=b_V)
        P.emit("pool", I(nc.gpsimd.memset, QT[:], 0.0), writes=b_QT)

        P.mark(1)
        P.emit("act", I(nc.scalar.activation, out=scT[:], in_=cT[:], func=AF.Silu), reads=[b_cT], writes=[b_scT])
        scT3 = scT[:].rearrange("p (k j) -> p k j", j=3)
        modps = gen[0]
        first = True
        wmod_last = {}
        for kc in range(8):
            for third in range(3):
                s = rr_xt.next()
                xtb = xt[s][:].bitcast(BF16)[:, 0:1024]
                tk = P.dma("pool", I(nc.gpsimd.dma_start, out=xtb, in_=wmod_d[kc * 128:(kc + 1) * 128, third * 1024:(third + 1) * 1024]),
                           f"wm{s}", writes=[b_xt[s]])
                wmod_last[s] = tk
                fns = []
                for jj in range(8):
                    col = (third * 8 + jj) * 3
                    fns.append(I(nc.tensor.matmul, modps[:, col:col + 3], lhsT=xtb[:, jj * 128:(jj + 1) * 128],
                                 rhs=scT3[:, kc, :], start=first, stop=(kc == 7 and third == 2 and jj == 7),
                                 skip_group_check=True))
                    first = False
                P.emit("pe", fns, reads=[b_xt[s], b_scT], writes=[b_gen[0]])
        wmod_toks = list(wmod_last.values())
        for i in (1, 2, 3):
            load_w(i)
        P.dma("pool", I(nc.gpsimd.dma_start, out=wpw[:], in_=wpw_d.rearrange("(cc p) n -> p cc n", p=128)),
              "w_pw", writes=[b_wpw])
        P.mark(2)
        P.emit("dve", I(nc.vector.tensor_tensor, out=modsb[:], in0=modps[:, 0:72].rearrange("p (c j) -> p c j", j=3),
                        in1=vcol(V_BMOD, 24).unsqueeze(2).to_broadcast([128, 24, 3]), op=ALU.add),
               reads=[b_gen[0], b_vecs], writes=[b_modsb])
        P.emit("dve", I(nc.vector.scalar_tensor_tensor, out=Asb[:], in0=modsb[:, 8:16, :], scalar=1.0,
                        in1=vcol(V_NORMW, 8).unsqueeze(2).to_broadcast([128, 8, 3]), op0=ALU.add, op1=ALU.mult),
               reads=[b_modsb, b_vecs], writes=[b_A])
        def Acol(kc, j):
            return Asb[:, kc, j:j + 1]

        def Bcol(kc, j):
            return modsb[:, kc, j:j + 1]

        def xphase(srcs, sq_eng="dve", rstd_eng="act"):
            for tt, src in enumerate(srcs):
                s = rr_xt.next()
                P.dma("sp", I(nc.sync.dma_start, out=xt[s][:], in_=src), f"xt{s}", writes=[b_xt[s]])
                q = rr_stat.next()
                if sq_eng == "act" or (sq_eng == "mix" and tt % 2 == 0):
                    P.emit("act", I(nc.scalar.activation, out=y2f[:, 0:1024], in_=xt[s][:], func=AF.Square,
                                    accum_out=stat[:, q, 0:1]), reads=[b_xt[s]], writes=b_y2[0:2] + [b_stat[q]])
                else:
                    P.emit("dve", I(nc.vector.scalar_tensor_tensor, out=y2f[:, 0:1024], in0=xt[s][:], scalar=1.0, in1=xt[s][:],
                                    op0=ALU.mult, op1=ALU.mult, accum_out=stat[:, q, 0:1]),
                           reads=[b_xt[s]], writes=b_y2[0:2] + [b_stat[q]])
                if rstd_eng == "pool":
                    P.emit("pool", I(nc.gpsimd.tensor_scalar, out=stat[:, q, 1:2], in0=stat[:, q, 0:1], scalar1=1.0 / 1024.0,
                                     scalar2=EPS, op0=ALU.mult, op1=ALU.add), reads=[b_stat[q]], writes=[b_stat[q]])
                    P.emit("pool", I(nc.gpsimd.tensor_tensor, out=stat[:, q, 2:3], in0=stat[:, q, 1:2], in1=epsb[:, 1:2], op=ALU.pow),
                           reads=[b_stat[q], b_eps], writes=[b_stat[q]])
                else:
                    P.emit("act", I(nc.scalar.activation, out=stat[:, q, 1:2], in_=stat[:, q, 0:1], func=AF.Ln,
                                    scale=1.0 / 1024.0, bias=epsb[:, 0:1]), reads=[b_stat[q], b_eps], writes=[b_stat[q]])
                    P.emit("act", I(nc.scalar.activation, out=stat[:, q, 2:3], in_=stat[:, q, 1:2], func=AF.Exp,
                                    scale=-0.5), reads=[b_stat[q]], writes=[b_stat[q]])
                P.emit("pool", I(nc.gpsimd.tensor_scalar, out=xn[:, tt, :], in0=xt[s][:], scalar1=stat[:, q, 2:3],
                                 scalar2=0.0, op0=ALU.mult, op1=ALU.add),
                       reads=[b_xt[s], b_stat[q]], writes=[b_xn[tt]])

        def tphase(nt, j):
            for kc in range(8):
                g = rr_T.next()
                tb = Tbs[g]
                fns = [I(nc.tensor.transpose, out=tb[:, tt * 128:(tt + 1) * 128], in_=xn[:, tt, kc * 128:(kc + 1) * 128],
                         identity=ident) for tt in range(nt)]
                P.emit("pe", fns, reads=b_xn[:nt] + [b_cbf], writes=[b_T[g]])
                P.emit("dve", I(nc.vector.tensor_scalar, out=uT[:, kc, 0:nt * 128], in0=tb[:, 0:nt * 128],
                                scalar1=Acol(kc, j), scalar2=Bcol(kc, j), op0=ALU.mult, op1=ALU.add),
                       reads=[b_T[g], b_A, b_modsb], writes=[b_uT[kc]])

        def proj(lo, N, wbufs):
            g = rr_gen.next()
            fns = [I(nc.tensor.matmul, gen[g][:, 0:N], lhsT=win[:, kc, lo:lo + 128], rhs=uT[:, kc, 0:N],
                     start=(kc == 0), stop=(kc == 7)) for kc in range(8)]
            P.emit("pe", fns, reads=b_uT + wbufs, writes=[b_gen[g]])
            return g

        def qk_A(g, N, L):
            P.emit("dve", I(nc.vector.tensor_copy, out=fs[L][:, 0:N], in_=gen[g][:, 0:N]),
                   reads=[b_gen[g]], writes=[b_fs[L]])
            P.emit("pool", I(nc.gpsimd.tensor_tensor, out=bs[L][:, 0:N], in0=fs[L][:, 0:N], in1=fs[L][:, 0:N], op=ALU.mult),
                   reads=[b_fs[L]], writes=[b_bs[L]])

        def qk_B(N, L):
            m = rr_gen.next()
            P.emit("pe", I(nc.tensor.matmul, gen[m][:, 0:N], lhsT=bones, rhs=bs[L][:, 0:N], start=True, stop=True),
                   reads=[b_bs[L], b_cbf], writes=[b_gen[m]])
            P.emit("act", I(nc.scalar.activation, out=fs[4 + L][:, 0:N], in_=gen[m][:, 0:N], func=AF.Ln, bias=epsb[:, 0:1]),
                   reads=[b_gen[m], b_eps], writes=[b_fs[4 + L]])
            P.emit("act", I(nc.scalar.activation, out=fs[4 + L][:, 0:N], in_=fs[4 + L][:, 0:N], func=AF.Exp, scale=-0.5),
                   reads=[b_fs[4 + L]], writes=[b_fs[4 + L]])

        def qk_C(N, L, wcol, rope, dests):
            raw, r = L, 4 + L
            if not rope:
                for rows, dest, dbuf in dests:
                    P.emit("dve", I(nc.vector.scalar_tensor_tensor, out=dest, in0=fs[raw][rows, 0:N], scalar=wcol[rows, :],
                                    in1=fs[r][rows, 0:N], op0=ALU.mult, op1=ALU.mult),
                           reads=[b_fs[raw], b_fs[r], b_vecs], writes=[dbuf])
                return
            P.emit("dve", I(nc.vector.scalar_tensor_tensor, out=bs[L][:, 0:N], in0=fs[raw][:, 0:N], scalar=wcol,
                            in1=fs[r][:, 0:N], op0=ALU.mult, op1=ALU.mult),
                   reads=[b_fs[raw], b_fs[r], b_vecs], writes=[b_bs[L]])
            m = rr_gen.next()
            P.emit("pe", I(nc.tensor.matmul, gen[m][:, 0:N], lhsT=Rm, rhs=bs[L][:, 0:N], start=True, stop=True),
                   reads=[b_bs[L], b_cbf], writes=[b_gen[m]])
            P.emit("pool", I(nc.gpsimd.tensor_tensor, out=fs[raw][:, 0:N], in0=bs[L][:, 0:N], in1=cosb[:, 0:N], op=ALU.mult),
                   reads=[b_bs[L], b_cos], writes=[b_fs[raw]])
            P.emit("dve", I(nc.vector.tensor_tensor, out=fs[r][:, 0:N], in0=gen[m][:, 0:N], in1=sinb[:, 0:N], op=ALU.mult),
                   reads=[b_gen[m], b_sin], writes=[b_fs[r]])
            for rows, dest, dbuf in dests:
                P.emit("pool", I(nc.gpsimd.tensor_tensor, out=dest, in0=fs[raw][rows, 0:N], in1=fs[r][rows, 0:N], op=ALU.add),
                       reads=[b_fs[raw], b_fs[r]], writes=[dbuf])

        def load_rope(g):
            P.dma("sp", I(nc.sync.dma_start, out=cosb[:], in_=cos_d[:, g * 512:(g + 1) * 512]), "cosb", writes=[b_cos])
            P.dma("sp", I(nc.sync.dma_start, out=sinb[:], in_=sin_d[:, g * 512:(g + 1) * 512]), "sinb", writes=[b_sin])

        def kv_front(nt, j, koff, L, after_t=None):
            N = nt * 128
            tphase(nt, j)
            if after_t is not None:
                after_t()
            g = proj(512, N, [b_win[0]])
            qk_A(g, N, L)
            qk_B(N, L)
            for tt in range(nt):
                gv = rr_gen.next()
                fns = [I(nc.tensor.matmul, gen[gv][:, 0:128], lhsT=uT[:, kc, tt * 128:(tt + 1) * 128],
                         rhs=win[:, kc, 640:768], start=(kc == 0), stop=(kc == 7)) for kc in range(8)]
                P.emit("pe", fns, reads=b_uT + [b_win[0]], writes=[b_gen[gv]])
                tile = koff // 128 + tt
                P.emit("act", I(nc.scalar.copy, out=Vaug[:, tile, 0, 0:64], in_=gen[gv][:, 0:64]),
                       reads=[b_gen[gv]], writes=[b_V[tile]])
                P.emit("act", I(nc.scalar.copy, out=Vaug[:, tile, 1, 64:128], in_=gen[gv][:, 64:128]),
                       reads=[b_gen[gv]], writes=[b_V[tile]])

        def kv_back(nt, koff, kidx, rope, L):
            N = nt * 128
            qk_C(N, L, vcol(V_KNW), rope, [(slice(0, 128), KT[:, koff:koff + N], b_KT[kidx])])

        def attention(gp, hook=None, bg=()):
            NP = NKT // 2
            items = [(c, half, pr) for c in range(4) for half in range(2) for pr in range(NP)]
            sbank = {}

            def emitS(idx):
                c, half, pr = items[idx]
                s = rr_S.next()
                sbank[idx] = s
                fns = []
                kset = set()
                for e in range(2):
                    kt = 2 * pr + e
                    kset.add(0 if kt < 2 else 1 + (kt - 2) // 4)
                    fns.append(I(nc.tensor.matmul, Sb[s][:, e * 512:(e + 1) * 512], lhsT=KT[:, kt * 128:(kt + 1) * 128],
                                 rhs=QT[:, c, half, :], start=True, stop=True))
                P.emit("pe", fns, reads=[b_KT[k] for k in kset] + [b_QT[c]], writes=[b_S[s]])
            SFIRST = not os.environ.get("NOSFIRST")
            emitS(0)
            if SFIRST:
                emitS(1)
            o = None
            bg = list(bg)
            bg_every = max(1, (len(items) - 8) // max(1, len(bg))) if bg else 0
            for idx, (c, half, pr) in enumerate(items):
                if idx == 8 and hook is not None:
                    hook()
                if bg and idx >= 4 and (idx - 4) % bg_every == 0:
                    bg.pop(0)()
                if not SFIRST and idx + 1 < len(items):
                    emitS(idx + 1)
                s = sbank.pop(idx)
                p = rr_PT.next()
                P.emit("act", I(nc.scalar.activation, out=PT[p][:, :], in_=Sb[s][:, :], func=AF.Exp, scale=0.125),
                       reads=[b_S[s]], writes=[b_PT[p]])
                if pr == 0:
                    o = rr_O.next()
                fns = []
                for e in range(2):
                    kt = 2 * pr + e
                    fns.append(I(nc.tensor.matmul, Ob[o][:, :], lhsT=Vaug[:, kt, half, :], rhs=PT[p][:, e * 512:(e + 1) * 512],
                                 start=(kt == 0), stop=(kt == NKT - 1)))
                if SFIRST and idx + 2 < len(items):
                    emitS(idx + 2)
                P.emit("pe", fns, reads=[b_V[2 * pr], b_V[2 * pr + 1], b_PT[p]], writes=[b_O[o]])
                if pr == NP - 1:
                    orow = slice(half * 64, half * 64 + 64)
                    srow = slice((1 - half) * 64, (1 - half) * 64 + 64)
                    ri = rr_fs.next()
                    P.emit("dve", I(nc.vector.reciprocal, out=fs[ri][srow, :], in_=Ob[o][srow, :]),
                           reads=[b_O[o]], writes=[b_fs[ri]])
                    ot = rr_fs.next()
                    P.emit("dve", I(nc.vector.tensor_tensor, out=fs[ot][orow, :], in0=Ob[o][orow, :], in1=fs[ri][srow, :],
                                    op=ALU.mult), reads=[b_O[o], b_fs[ri]], writes=[b_fs[ot]])
                    P.emit("pool", I(nc.gpsimd.tensor_tensor, out=za[gp][orow, c, :], in0=fs[ot][orow, :],
                                     in1=za[gp][orow, c, :], op=ALU.mult), reads=[b_fs[ot]], writes=[b_za[gp][c]])
            while bg:
                bg.pop(0)()

        cst = {}

        def build_diag(cc):
            d = rr_diag.next()
            fns = [I(nc.vector.tensor_scalar, out=diag[d][:, j, :], in0=ident, scalar1=vcol(V_CW + cc * 31 + j),
                     scalar2=None, op0=ALU.mult) for j in range(31)]
            P.emit("dve", fns, reads=[b_cbf, b_vecs], writes=[b_diag[d]])
            return d

        def prebuild_diags():
            cst["prebuilt"] = {0: build_diag(0), 1: build_diag(1)}

        def conv_block(gg, gp):
            conv_p1(gg)
            conv_p2a()
            conv_p2b(gp)

        def conv_p1(gg):
            mu_ps, msq_ps = Ob[0], Ob[1]
            for cc in range(4):
                if cc in cst.get("prebuilt", {}):
                    d = cst["prebuilt"].pop(cc)
                else:
                    d = build_diag(cc)
                g = rr_gen.next()
                fns = [I(nc.tensor.matmul, gen[g][:, :], lhsT=diag[d][:, j, :],
                         rhs=uconv[:, cc, gg * 512 + j: gg * 512 + j + 512], start=(j == 0), stop=(j == 30)) for j in range(31)]
                ucr = [b_uc[cc][k] for k in range(max(0, gg - 1), min(3, gg + 1) + 1)] + [b_ucpad]
                P.emit("pe", fns, reads=[b_diag[d]] + ucr, writes=[b_gen[g]])
                P.emit("act", I(nc.scalar.activation, out=ybf[:, cc, :], in_=gen[g][:, :], func=AF.Identity,
                                bias=vcol(V_CB + cc)), reads=[b_gen[g], b_vecs], writes=[b_ybf[cc]])
                sq = rr_bs.next()
                P.emit("pool", I(nc.gpsimd.tensor_tensor, out=bs[sq][:, :], in0=ybf[:, cc, :], in1=ybf[:, cc, :], op=ALU.mult),
                       reads=[b_ybf[cc]], writes=[b_bs[sq]])
                P.emit("pe", I(nc.tensor.matmul, mu_ps[:, :], lhsT=odiv, rhs=ybf[:, cc, :], start=(cc == 0), stop=(cc == 3)),
                       reads=[b_ybf[cc], b_cbf], writes=[b_O[0]])
                P.emit("pe", I(nc.tensor.matmul, msq_ps[:, :], lhsT=odiv, rhs=bs[sq][:, :], start=(cc == 0), stop=(cc == 3)),
                       reads=[b_bs[sq], b_cbf], writes=[b_O[1]])

        def conv_p2a():
            mu_ps, msq_ps = Ob[0], Ob[1]
            mu = rr_fs.next()
            P.emit("act", I(nc.scalar.copy, out=fs[mu][:, :], in_=mu_ps[:, :]), reads=[b_O[0]], writes=[b_fs[mu]])
            m2 = rr_fs.next()
            cst["mu"], cst["m2"] = mu, m2
            P.emit("pool", I(nc.gpsimd.tensor_tensor, out=fs[m2][:, :], in0=fs[mu][:, :], in1=fs[mu][:, :], op=ALU.mult),
                   reads=[b_fs[mu]], writes=[b_fs[m2]])
            P.emit("dve", I(nc.vector.tensor_tensor, out=fs[m2][:, :], in0=msq_ps[:, :], in1=fs[m2][:, :], op=ALU.subtract),
                   reads=[b_O[1], b_fs[m2]], writes=[b_fs[m2]])
            P.emit("act", I(nc.scalar.activation, out=fs[m2][:, :], in_=fs[m2][:, :], func=AF.Ln, bias=epsb[:, 0:1]),
                   reads=[b_fs[m2], b_eps], writes=[b_fs[m2]])
            P.emit("act", I(nc.scalar.activation, out=fs[m2][:, :], in_=fs[m2][:, :], func=AF.Exp, scale=-0.5),
                   reads=[b_fs[m2]], writes=[b_fs[m2]])

        def conv_p2b(gp):
            mu, m2 = cst["mu"], cst["m2"]
            g0, g1 = rr_gen.next(), rr_gen.next()
            banks = [(gen[g0], b_gen[g0]), (gen[g1], b_gen[g1]), (Ob[0], b_O[0]), (Ob[1], b_O[1])]
            for cc in range(4):
                t = rr_fs.next()
                P.emit("dve", I(nc.vector.tensor_tensor, out=fs[t][:, :], in0=ybf[:, cc, :], in1=fs[mu][:, :], op=ALU.subtract),
                       reads=[b_ybf[cc], b_fs[mu]], writes=[b_fs[t]])
                P.emit("dve", I(nc.vector.tensor_tensor, out=fs[t][:, :], in0=fs[t][:, :], in1=fs[m2][:, :], op=ALU.mult),
                       reads=[b_fs[t], b_fs[m2]], writes=[b_fs[t]])
                P.emit("act", I(nc.scalar.activation, out=y2[:, cc, :], in_=fs[t][:, :], func=AF.Silu,
                                scale=vcol(V_LNW + cc), bias=vcol(V_LNB + cc)), reads=[b_fs[t], b_vecs], writes=[b_y2[cc]])
                fns = [I(nc.tensor.matmul, banks[nn][0][:, :], lhsT=wpw[:, cc, nn * 128:(nn + 1) * 128], rhs=y2[:, cc, :],
                         start=(cc == 0), stop=(cc == 3)) for nn in range(4)]
                P.emit("pe", fns, reads=[b_y2[cc], b_wpw], writes=[bk[1] for bk in banks])
            for nn in range(4):
                P.emit("dve", I(nc.vector.scalar_tensor_tensor, out=zc[gp][:, nn, :], in0=banks[nn][0][:, :], scalar=vcol(V_BPW + nn),
                                in1=zc[gp][:, nn, :], op0=ALU.add, op1=ALU.mult),
                       reads=[banks[nn][1], b_vecs], writes=[b_zc[gp][nn]])

        store_toks = {}

        def outproj(bi, gg, gp):
            for tt in range(4):
                r0 = gg * 512 + tt * 128
                for nh in range(2):
                    xr = rr_xres.next()
                    P.dma("sp", I(nc.sync.dma_start, out=xres[xr][:], in_=x_d[bi, r0:r0 + 128, nh * 512:(nh + 1) * 512]),
                          f"xr{xr}", writes=[b_xres[xr]])
                    g = rr_gen.next()
                    fns = []
                    for dch in range(8):
                        src = za[gp][:, dch, tt * 128:(tt + 1) * 128] if dch < 4 else zc[gp][:, dch - 4, tt * 128:(tt + 1) * 128]
                        fns.append(I(nc.tensor.matmul, gen[g][:, :], lhsT=src, rhs=wout[:, dch, nh * 512:(nh + 1) * 512],
                                     start=(dch == 0), stop=(dch == 7)))
                    P.emit("pe", fns, reads=b_za[gp] + b_zc[gp] + b_wout, writes=[b_gen[g]])
                    P.emit("dve", I(nc.vector.tensor_tensor, out=xres[xr][:], in0=gen[g][:, :], in1=xres[xr][:], op=ALU.add),
                           reads=[b_gen[g]], writes=[b_xres[xr]])
                    store_toks[xr] = P.dma("pool", I(nc.gpsimd.dma_start, out=out_d[bi, r0:r0 + 128, nh * 512:(nh + 1) * 512],
                                                     in_=xres[xr][:]), f"st{xr}", reads=[b_xres[xr]])

        epsb = sb("epsb", [128, 2], F32)
        b_eps = Buf()
        P.emit("pool", I(nc.gpsimd.memset, epsb[:, 0:1], EPS), writes=[b_eps])
        P.emit("pool", I(nc.gpsimd.memset, epsb[:, 1:2], -0.5), writes=[b_eps])

        for bi in range(2):
            P.mark(3 + 10 * bi)
            on = 0
            P.emit("pool", I(nc.gpsimd.memset, fs[on][:, 0:128], 1.0), writes=[b_fs[on]])
            for nh in range(2):
                for q4 in range(4):
                    ch = nh * 4 + q4
                    dg = 1 + (ch % 7)
                    P.emit("dve", I(nc.vector.tensor_scalar, out=fs[dg][:, 0:128], in0=ident, scalar1=modsb[:, 16 + ch, bi:bi + 1],
                                    scalar2=None, op0=ALU.mult), reads=[b_cbf, b_modsb], writes=[b_fs[dg]])
                    P.emit("pe", I(nc.tensor.matmul, Ob[nh][:, q4 * 128:(q4 + 1) * 128], lhsT=fs[on][:, 0:128], rhs=fs[dg][:, 0:128],
                                   start=True, stop=True), reads=[b_fs[on], b_fs[dg]], writes=[b_O[nh]])
            WST = [2, 3, 6, 7]

            def wout_dma(k):
                for kc in (2 * k, 2 * k + 1):
                    for nh in range(2):
                        st_ = WST[(kc % 2) * 2 + nh]
                        P.dma("sp", I(nc.sync.dma_start, out=fs[st_][:, :], in_=wout_d[kc * 128:(kc + 1) * 128, nh * 512:(nh + 1) * 512]),
                              f"wst{st_}", writes=[b_fs[st_]])

            def wout_mul(k):
                for kc in (2 * k, 2 * k + 1):
                    for nh in range(2):
                        st_ = WST[(kc % 2) * 2 + nh]
                        P.emit("dve", I(nc.vector.tensor_tensor, out=wout[:, kc, nh * 512:(nh + 1) * 512],
                                        in0=Ob[nh][:, :], in1=fs[st_][:, :], op=ALU.mult),
                               reads=[b_O[nh], b_fs[st_]], writes=[b_wout[kc]])

            P.mark(4 + 10 * bi)
            def lat_srcs(g):
                return [x_d[bi, g * 512 + tt * 128: g * 512 + (tt + 1) * 128, :] for tt in range(4)]
            kvg = [(2, 2, 0, 0, False, [ctx_d[bi, tt * 128:(tt + 1) * 128, :] for tt in range(2)])]
            for g in range(4):
                kvg.append((4, bi, 256 + g * 512, 1 + g, True, lat_srcs(g)))
            xphase(kvg[0][5], "mix")
            prev = None
            for i, (nt, j, koff, kidx, rope, srcs) in enumerate(kvg):
                L = i % 2
                nxt = kvg[i + 1][5] if i + 1 < len(kvg) else lat_srcs(0)
                kv_front(nt, j, koff, L, after_t=lambda nxt=nxt: xphase(nxt, "mix"))
                if i >= 1:
                    wout_mul(i - 1)
                if i <= 3:
                    wout_dma(i)
                if rope:
                    load_rope(i - 1)
                kv_back(nt, koff, kidx, rope, L)

            if dbg and bi == 0:
                def dcp(name, src_ap, n, bufs):
                    t = 0
                    while t < n:
                        k = rr_fs.next()
                        P.emit("dve", I(nc.vector.tensor_copy, out=fs[k][:, :], in_=src_ap[:, t:t + 512]), reads=bufs, writes=[b_fs[k]])
                        P.dma("sp", I(nc.sync.dma_start, out=dbg_d[name][:, t:t + 512], in_=fs[k][:, :]), "dbg", reads=[b_fs[k]])
                        t += 512
                dcp("d_KT", KT[:, 0:2048], 2048, b_KT)
                dcp("d_V", Vaug[:].rearrange("p a b c -> p (a b c)"), NKT * 256, b_V)

            P.mark(6 + 10 * bi)
            def side_units(g):
                gp_ = g % 2
                units = []
                for cc in range(4):
                    def u_g(cc=cc):
                        gg_ = proj(1792 + cc * 128, 512, [b_win[3]])
                        sg = 4 + (cc % 2)
                        P.emit("act", I(nc.scalar.activation, out=bs[sg][:, :], in_=gen[gg_][:, :], func=AF.Sigmoid),
                               reads=[b_gen[gg_]], writes=[b_bs[sg]])
                        ga = proj(1280 + cc * 128, 512, [b_win[2]])
                        P.emit("dve", I(nc.vector.tensor_tensor, out=uconv[:, cc, 15 + g * 512: 15 + (g + 1) * 512],
                                        in0=gen[ga][:, :], in1=bs[sg][:, :], op=ALU.mult),
                               reads=[b_gen[ga], b_bs[sg]], writes=[b_uc[cc][g]])
                    units.append(u_g)
                for cc in range(4):
                    def u_zc(cc=cc):
                        gz = proj(2304 + cc * 128, 512, [b_win[3]])
                        P.emit("act", I(nc.scalar.activation, out=zc[gp_][:, cc, :], in_=gen[gz][:, :], func=AF.Silu),
                               reads=[b_gen[gz]], writes=[b_zc[gp_][cc]])
                    units.append(u_zc)
                for c in range(4):
                    def u_za(c=c):
                        gz = proj(768 + c * 128, 512, [b_win[2]])
                        P.emit("act", I(nc.scalar.activation, out=za[gp_][:, c, :], in_=gen[gz][:, :], func=AF.Silu),
                               reads=[b_gen[gz]], writes=[b_za[gp_][c]])
                    units.append(u_za)
                return units

            tphase(4, bi)
            xphase(lat_srcs(1))
            for g in range(4):
                gp = g % 2
                load_rope(g)
                for c in range(4):
                    gq = proj(c * 128, 512, [b_win[1]])
                    qk_A(gq, 512, c)
                for c in range(4):
                    qk_B(512, c)
                if g >= 1:
                    prebuild_diags()
                su = side_units(g)
                for u in su[0:4]:
                    u()
                for c in range(4):
                    qk_C(512, c, vcol(V_QNW), True, [(slice(0, 64), QT[0:64, c, 0, :], b_QT[c]),
                                                     (slice(64, 128), QT[64:128, c, 1, :], b_QT[c])])
                if dbg and bi == 0 and g == 0:
                    dcp("d_QT", QT[:].rearrange("p a b c -> p (a b c)"), 4096, b_QT)
                P.mark(7 + 10 * bi)
                if g >= 1:
                    conv_p1(g - 1)
                for u in su[4:8]:
                    u()
                if g >= 1:
                    conv_p2a()
                for u in su[8:12]:
                    u()
                if g < 3:
                    tphase(4, bi)
                if g >= 1:
                    conv_p2b(1 - gp)
                    if dbg and bi == 0 and g == 1:
                        dcp("d_zc", zc[1 - gp][:].rearrange("p a b -> p (a b)"), 2048, b_zc[1 - gp])
                        dcp("d_za", za[1 - gp][:].rearrange("p a b -> p (a b)"), 2048, b_za[1 - gp])
                    outproj(bi, g - 1, 1 - gp)
                P.mark(8 + 10 * bi)
                hook = (lambda gn=g + 2: xphase(lat_srcs(gn), "dve", "pool")) if g + 2 <= 3 else None
                attention(gp, hook)
                P.mark(6 + 10 * bi)
            P.mark(9 + 10 * bi)
            conv_block(3, 1)
            outproj(bi, 3, 1)

        P.enabled = True
        P.wait_only("sp", [t for t in store_toks.values() if t])
        P.wait_only("pool", [t for t in store_toks.values() if t])
        P.build(nc, st)
    return nc


def _perm_heads():
    idx = []
    for c in range(4):
        for half in range(2):
            h = c + 4 * half
            idx.extend(range(h * 64, (h + 1) * 64))
    return np.array(idx)


def _rope_tables():
    S = S_LAT
    t = np.arange(S)
    row = (t // 64).astype(np.float64)
    col = (t % 64).astype(np.float64)
    freqs = 10000.0 ** (-np.arange(0, 32, 2, dtype=np.float64) / 32.0)
    ang_r = row[:, None] * freqs[None, :]
    ang_c = col[:, None] * freqs[None, :]
    cosT = np.zeros((128, S), np.float32)
    sinT = np.zeros((128, S), np.float32)
    for p in range(128):
        d = p % 64
        if d < 32:
            a = ang_r[:, d % 16]
        else:
            a = ang_c[:, (d - 32) % 16]
        cosT[p] = np.cos(a)
        sinT[p] = np.sin(a)
    return cosT, sinT


def _consts():
    c = np.zeros((128, 512), np.float32)
    c[:, 0:128] = np.eye(128, dtype=np.float32)
    R = np.zeros((128, 128), np.float32)
    for m in range(128):
        if (m % 32) < 16:
            R[m + 16, m] = -1.0
        else:
            R[m - 16, m] = 1.0
    c[:, 128:256] = R
    bo = np.zeros((128, 128), np.float32)
    bo[0:64, 0:64] = 1.0 / 64.0
    bo[64:128, 64:128] = 1.0 / 64.0
    c[:, 256:384] = bo
    c[:, 384:512] = 1.0 / 512.0
    sel = np.zeros((3, 256), np.float32)
    sel[0, 0:128] = 1.0
    sel[1, 128:256] = 1.0
    return c, sel


_NC_CACHE = {}


def _prep(inputs):
    f = lambda a: np.ascontiguousarray(np.asarray(a, dtype=np.float32))
    x = f(inputs["x"]); c = f(inputs["c"]); ctx = f(inputs["ctx"]); c_ctx = f(inputs["c_ctx"])
    w_mod = f(inputs["w_mod"])[0]; b_mod = f(inputs["b_mod"])[0]; norm_w = f(inputs["norm_w"])[0]
    w_in = f(inputs["w_in"])[0]; qn = f(inputs["q_norm_w"])[0]; kn = f(inputs["k_norm_w"])[0]
    conv_w = f(inputs["conv_w"])[0]; conv_b = f(inputs["conv_b"])[0]
    ln_w = f(inputs["conv_ln_w"])[0]; ln_b = f(inputs["conv_ln_b"])[0]
    w_pw = f(inputs["w_pw"])[0]; b_pw = f(inputs["b_pw"])[0]; w_out = f(inputs["w_out"])[0]
    perm = _perm_heads()
    cols = np.arange(2816)
    cols[0:512] = perm
    cols[768:1280] = 768 + perm
    w_in_p = np.ascontiguousarray(w_in[:, cols])
    rows = np.arange(1024)
    rows[0:512] = perm
    w_out_p = np.ascontiguousarray(w_out[rows, :])
    vecs = np.zeros((128, NV), np.float32)
    vecs[:, V_NORMW:V_NORMW + 8] = norm_w.reshape(8, 128).T
    vecs[:, V_BMOD:V_BMOD + 24] = b_mod.reshape(24, 128).T
    vecs[:, V_QNW] = np.tile(qn, 2)
    vecs[:, V_KNW] = np.tile(kn, 2)
    vecs[:, V_CB:V_CB + 4] = conv_b.reshape(4, 128).T
    vecs[:, V_LNW:V_LNW + 4] = ln_w.reshape(4, 128).T
    vecs[:, V_LNB:V_LNB + 4] = ln_b.reshape(4, 128).T
    vecs[:, V_BPW:V_BPW + 4] = b_pw.reshape(4, 128).T
    vecs[:, V_CW:V_CW + 124] = conv_w.reshape(31, 4, 128).transpose(2, 1, 0).reshape(128, 124)
    constsf = np.concatenate([np.eye(128, dtype=np.float32), np.ones((128, 128), np.float32)], axis=1)
    consts, sel = _consts()
    cosT, sinT = _rope_tables()
    in_maps = []
    for i in range(8):
        cc = np.stack([c[2 * i], c[2 * i + 1], c_ctx], axis=1)
        cT = np.ascontiguousarray(cc.reshape(8, 128, 3).transpose(1, 0, 2).reshape(128, 24))
        in_maps.append({
            "x": np.ascontiguousarray(x[2 * i:2 * i + 2]), "ctx": np.ascontiguousarray(ctx[2 * i:2 * i + 2]),
            "cT": cT, "w_mod": w_mod, "w_in": w_in_p, "w_pw": w_pw, "w_out": w_out_p, "vecs": vecs,
            "consts": consts, "cosT": cosT, "sinT": sinT,
        })
    return in_maps


def kernel(**inputs):
    in_maps = _prep(inputs)
    if "nc" not in _NC_CACHE:
        _NC_CACHE["nc"] = build_nc()
    nc = _NC_CACHE["nc"]
    res = run_bass_kernel_spmd(nc, in_maps, core_ids=list(range(8)))
    return np.concatenate([np.asarray(r["out"], dtype=np.float32) for r in res.results], axis=0)
```
